# Optimizing a Trainium2 kernel written in Bass

```python
import jax
import jax.numpy as jnp
from jax import lax
import numpy as np

D_MODEL = 1024
BATCH = 32
SEQ = 2048
DEPTH = 4

CTX_LEN = 256
GRID_W = 64
ROPE_THETA = 10000.0
EPS = 1e-6
NEG_INF = -1e30
F32 = jnp.float32
ATTN_BLOCK = 128

MLA_HEADS = 4
MLA_Q_LORA = 256
MLA_KV_LORA = 256
MLA_NOPE = 128
MLA_ROPE = 64
MLA_V = 128
CMLP_GROUPS = 4
CMLP_GROUP_DIM = 128
CMLP_CHUNK = 128
CMLP_WIDTH = CMLP_GROUPS * CMLP_GROUP_DIM
RET_HEADS = 4
RET_QK = 64
RET_V = 128
RET_CHUNK = 128
RET_QK_W = RET_HEADS * RET_QK
RET_V_W = RET_HEADS * RET_V
SWA_Q_HEADS = 8
SWA_KV_HEADS = 2
SWA_HEAD_DIM = 64
SWA_WINDOW = 128
SWA_BLOCK = 128
SWA_Q_W = SWA_Q_HEADS * SWA_HEAD_DIM
SWA_KV_W = SWA_KV_HEADS * SWA_HEAD_DIM
FFN_HIDDEN = -(-8 * D_MODEL // (3 * 256)) * 256

AB_KV_WIDTH = MLA_KV_LORA + MLA_ROPE
AB_IN = AB_KV_WIDTH + MLA_Q_LORA + 2 * CMLP_WIDTH
AB_OUT = MLA_HEADS * MLA_V + CMLP_WIDTH
CD_KV_WIDTH = RET_QK_W + RET_V_W + 2 * SWA_KV_W
CD_IN = CD_KV_WIDTH + RET_QK_W + RET_V_W + SWA_Q_W
CD_OUT = RET_V_W + SWA_Q_W

N_EVEN = (DEPTH + 1) // 2
N_ODD = DEPTH // 2

kernel_name = 'hybrid_mla_gmlp_retention_swa_dit_trunk'


def _rms(x, g):
    xf = x.astype(F32)
    y = xf * lax.rsqrt(jnp.mean(xf * xf, axis=-1, keepdims=True) + EPS)
    return (y * g.astype(F32)).astype(x.dtype)


def _group_rms(y, g, groups):
    shp = y.shape
    yf = y.astype(F32).reshape(shp[:-1] + (groups, shp[-1] // groups))
    yf = yf * lax.rsqrt(jnp.mean(yf * yf, axis=-1, keepdims=True) + EPS)
    return (yf.reshape(shp) * g.astype(F32)).astype(y.dtype)


def _modulate(x, shift, scale):
    return x * (1.0 + scale) + shift


def _split_cols(z, widths):
    return jnp.split(z, np.cumsum(widths)[:-1].tolist(), axis=-1)


def _heads(a, h):
    return a.reshape(a.shape[:2] + (h, a.shape[-1] // h))


def _flip(a):
    return jnp.flip(a, axis=1)


def _axial_rope(n, rot_dim):
    t = jnp.arange(n)
    row = (t // GRID_W).astype(F32)
    col = (t % GRID_W).astype(F32)
    n_freq = rot_dim // 4
    freqs = ROPE_THETA ** (-jnp.arange(n_freq, dtype=F32) / n_freq)
    ang = jnp.concatenate([row[:, None] * freqs, col[:, None] * freqs], axis=-1)
    return jnp.cos(ang), jnp.sin(ang)


def _apply_rope(x, rope):
    cos, sin = rope
    cos = cos[None, :, None, :].astype(x.dtype)
    sin = sin[None, :, None, :].astype(x.dtype)
    x1, x2 = jnp.split(x, 2, axis=-1)
    return jnp.concatenate([x1 * cos - x2 * sin, x1 * sin + x2 * cos], axis=-1)


def _to_blocks(a, blk):
    B, n = a.shape[:2]
    return a.reshape((B, n // blk, blk) + a.shape[2:]).swapaxes(0, 1)


def _from_blocks(a):
    nb, B, blk = a.shape[:3]
    return a.swapaxes(0, 1).reshape((B, nb * blk) + a.shape[3:])


def _attend_block(q, k, v, scale, mask=None, sink=None):
    s = jnp.einsum('bqhgd,bkhd->bhgqk', q, k, preferred_element_type=F32) * scale
    if mask is not None:
        s = jnp.where(mask, s, NEG_INF)
    if sink is not None:
        sink_col = jnp.broadcast_to(sink.astype(F32)[None, :, :, None, None], s.shape[:-1] + (1,))
        p = jax.nn.softmax(jnp.concatenate([s, sink_col], axis=-1), axis=-1)[..., :-1]
    else:
        p = jax.nn.softmax(s, axis=-1)
    return jnp.einsum('bhgqk,bkhe->bqhge', p.astype(v.dtype), v)


def _dense_attention(q, k, v, scale, sink=None):
    out = lax.map(lambda qb: _attend_block(qb, k, v, scale, sink=sink), _to_blocks(q, ATTN_BLOCK))
    return _from_blocks(out)


def _window_attention(q, k, v, k_ctx, v_ctx, scale, sink):
    n = q.shape[1]
    blk = SWA_BLOCK
    nb = n // blk
    pad = ((0, 0), (blk, blk), (0, 0), (0, 0))
    k_pad = jnp.pad(k, pad)
    v_pad = jnp.pad(v, pad)
    rel_k = jnp.arange(3 * blk) - blk
    band = jnp.abs(rel_k[None, :] - jnp.arange(blk)[:, None]) <= SWA_WINDOW
    ctx_mask = jnp.ones((blk, k_ctx.shape[1]), dtype=bool)

    def one_block(args):
        i, qb = args
        kpos = i * blk + rel_k
        local = band & ((kpos >= 0) & (kpos < n))[None, :]
        kb = lax.dynamic_slice_in_dim(k_pad, i * blk, 3 * blk, axis=1)
        vb = lax.dynamic_slice_in_dim(v_pad, i * blk, 3 * blk, axis=1)
        return _attend_block(qb, jnp.concatenate([kb, k_ctx], axis=1), jnp.concatenate([vb, v_ctx], axis=1),
                             scale, jnp.concatenate([local, ctx_mask], axis=1), sink)

    out = lax.map(one_block, (jnp.arange(nb), _to_blocks(q, blk)))
    return _from_blocks(out)


def _retention_chunkwise(q, k, v, log_gamma, s0):
    C = RET_CHUNK
    idx = jnp.arange(C, dtype=F32)
    diff = idx[:, None] - idx[None, :]
    intra = jnp.where(diff[None] >= 0, jnp.exp(log_gamma[:, None, None] * jnp.maximum(diff, 0.0)[None]), 0.0)
    q_dec = jnp.exp((idx[:, None] + 1.0) * log_gamma[None, :])[None, :, :, None]
    k_dec = jnp.exp((C - 1.0 - idx[:, None]) * log_gamma[None, :])[None, :, :, None]
    c_dec = jnp.exp(C * log_gamma)[None, :, None, None]

    def step(s, inp):
        qc, kc, vc = inp
        vc = vc.astype(F32)
        att = jnp.einsum('bqhd,bkhd->bhqk', qc, kc, preferred_element_type=F32) * intra[None]
        y = jnp.einsum('bhqk,bkhe->bqhe', att, vc) + jnp.einsum('bqhd,bhde->bqhe', qc.astype(F32), s) * q_dec
        s = s * c_dec + jnp.einsum('bkhd,bkhe->bhde', kc.astype(F32) * k_dec, vc)
        return s, y

    s, ys = lax.scan(step, s0, (_to_blocks(q, C), _to_blocks(k, C), _to_blocks(v, C)))
    return _from_blocks(ys), s


def _retention_state(k, v, log_gamma):
    n = k.shape[1]
    w = jnp.exp((n - 1.0 - jnp.arange(n, dtype=F32))[:, None] * log_gamma[None, :])
    return jnp.einsum('bjhd,bjhe,jh->bhde', k.astype(F32), v.astype(F32), w)


def _mixer_ab(xn, hn, w_in, w_out, q_norm, kv_norm, wq_b, wkv_b, v_norm, w_s, b_s, with_ctx_out):
    n = xn.shape[1]
    rope = _axial_rope(n, MLA_ROPE)
    scale = (MLA_NOPE + MLA_ROPE) ** -0.5

    def kv_side(z, rope_):
        kv_lat, k_pe = _split_cols(z[..., :AB_KV_WIDTH], [MLA_KV_LORA, MLA_ROPE])
        kv = _heads(_rms(kv_lat, kv_norm) @ wkv_b, MLA_HEADS)
        k_nope, v = jnp.split(kv, [MLA_NOPE], axis=-1)
        k_pe = k_pe[:, :, None, :]
        if rope_ is not None:
            k_pe = _apply_rope(k_pe, rope_)
        k = jnp.concatenate([k_nope, jnp.broadcast_to(k_pe, k_nope.shape[:3] + (MLA_ROPE,))], axis=-1)
        return k, v

    def q_side(z, rope_):
        q_lat = z[..., AB_KV_WIDTH:AB_KV_WIDTH + MLA_Q_LORA]
        q = _heads(_rms(q_lat, q_norm) @ wq_b, MLA_HEADS)
        if rope_ is not None:
            q = jnp.concatenate([q[..., :MLA_NOPE], _apply_rope(q[..., MLA_NOPE:], rope_)], axis=-1)
        return q[:, :, :, None, :]

    def chunk_mlp(z):
        u, v = jnp.split(jax.nn.gelu(z[..., AB_KV_WIDTH + MLA_Q_LORA:]), 2, axis=-1)
        v = _group_rms(v, v_norm, CMLP_GROUPS)
        B, m = v.shape[:2]
        v = v.reshape(B, m // CMLP_CHUNK, CMLP_CHUNK, CMLP_GROUPS, CMLP_GROUP_DIM)
        v = jnp.einsum('gpq,bcqgd->bcpgd', w_s, v) + b_s.T[None, None, :, :, None]
        return u * v.reshape(B, m, CMLP_WIDTH)

    def merge(o, z):
        o = o.reshape(o.shape[:2] + (MLA_HEADS * MLA_V,))
        return jnp.concatenate([o, chunk_mlp(z)], axis=-1) @ w_out

    zx = xn @ w_in
    zh = hn @ (w_in if with_ctx_out else w_in[:, :AB_KV_WIDTH])
    kx, vx = kv_side(zx, rope)
    kh, vh = kv_side(zh, None)
    ox = _dense_attention(q_side(zx, rope), jnp.concatenate([kx, kh], axis=1),
                          jnp.concatenate([vx, vh], axis=1), scale)
    yx = merge(ox, zx)
    yh = None
    if with_ctx_out:
        yh = merge(_dense_attention(q_side(zh, None), kh, vh, scale), zh)
    return yx, yh


def _mixer_cd(xn, hn, w_in, w_out, dec_f, dec_b, ret_norm, sink, with_ctx_out):
    B, n = xn.shape[:2]
    rope_ret = _axial_rope(n, RET_QK)
    rope_swa = _axial_rope(n, SWA_HEAD_DIM)
    lg_f = jax.nn.log_sigmoid(dec_f.astype(F32))
    lg_b = jax.nn.log_sigmoid(dec_b.astype(F32))
    swa_scale = SWA_HEAD_DIM ** -0.5
    groups = SWA_Q_HEADS // SWA_KV_HEADS
    sink_g = sink.reshape(SWA_KV_HEADS, groups)

    def kv_side(z, use_rope):
        rk, rv, sk, sv = _split_cols(z[..., :CD_KV_WIDTH], [RET_QK_W, RET_V_W, SWA_KV_W, SWA_KV_W])
        rk = _heads(rk, RET_HEADS) * (RET_QK ** -0.5)
        sk = _heads(sk, SWA_KV_HEADS)
        if use_rope:
            rk = _apply_rope(rk, rope_ret)
            sk = _apply_rope(sk, rope_swa)
        return rk, _heads(rv, RET_HEADS), sk, _heads(sv, SWA_KV_HEADS)

    def q_side(z, use_rope):
        rq, rg, sq = _split_cols(z[..., CD_KV_WIDTH:], [RET_QK_W, RET_V_W, SWA_Q_W])
        rq = _heads(rq, RET_HEADS)
        sq = _heads(sq, SWA_Q_HEADS)
        if use_rope:
            rq = _apply_rope(rq, rope_ret)
            sq = _apply_rope(sq, rope_swa)
        return rq, rg, sq.reshape(sq.shape[:2] + (SWA_KV_HEADS, groups, SWA_HEAD_DIM))

    def merge(y_ret, gate, o_swa, dtype):
        y_ret = _group_rms(y_ret.reshape(y_ret.shape[:2] + (RET_V_W,)), ret_norm, RET_HEADS).astype(dtype)
        y_ret = y_ret * jax.nn.silu(gate)
        return jnp.concatenate([y_ret, o_swa.reshape(o_swa.shape[:2] + (SWA_Q_W,))], axis=-1) @ w_out

    zx = xn @ w_in
    zh = hn @ (w_in if with_ctx_out else w_in[:, :CD_KV_WIDTH])
    rkh, rvh, skh, svh = kv_side(zh, False)
    yh = None
    if with_ctx_out:
        rqh, rgh, sqh = q_side(zh, False)
        s0 = jnp.zeros((B, RET_HEADS, RET_QK, RET_V), F32)
        yh_f, s_f = _retention_chunkwise(rqh, rkh, rvh, lg_f, s0)
        yh_b, s_b = _retention_chunkwise(_flip(rqh), _flip(rkh), _flip(rvh), lg_b, s0)
        yh = merge(yh_f + _flip(yh_b), rgh, _dense_attention(sqh, skh, svh, swa_scale, sink_g), hn.dtype)
    else:
        s_f = _retention_state(rkh, rvh, lg_f)
        s_b = _retention_state(_flip(rkh), _flip(rvh), lg_b)
    rkx, rvx, skx, svx = kv_side(zx, True)
    rqx, rgx, sqx = q_side(zx, True)
    yx_f, _ = _retention_chunkwise(rqx, rkx, rvx, lg_f, s_f)
    yx_b, _ = _retention_chunkwise(_flip(rqx), _flip(rkx), _flip(rvx), lg_b, s_b)
    ox = _window_attention(sqx, skx, svx, skh, svh, swa_scale, sink_g)
    yx = merge(yx_f + _flip(yx_b), rgx, ox, xn.dtype)
    return yx, yh


def _swiglu(x, w_in, w_out):
    a, b = jnp.split(x @ w_in, 2, axis=-1)
    return (jax.nn.silu(a) * b) @ w_out


def setup_inputs(seed: int = 0) -> dict:
    key = jax.random.key(seed)
    ks = iter(jax.random.split(key, 32))
    D = D_MODEL

    def nrm(shape, s):
        return jax.random.normal(next(ks), shape, F32) * s

    decay_logit = jnp.asarray(np.log(2.0 ** (5 + np.arange(RET_HEADS)) - 1.0), dtype=F32)
    return dict(
        x=nrm((BATCH, SEQ, D), 1.0),
        c=nrm((BATCH, D), 1.0),
        ctx=nrm((BATCH, CTX_LEN, D), 1.0),
        c_ctx=nrm((D,), 1.0),
        ada_w=nrm((DEPTH, D, 6 * D), 0.5 * D ** -0.5),
        ada_b=nrm((DEPTH, 6 * D), 0.02),
        norm_mix=1.0 + nrm((DEPTH, D), 0.02),
        norm_ffn=1.0 + nrm((DEPTH, D), 0.02),
        norm_final=1.0 + nrm((D,), 0.02),
        ffn_in=nrm((DEPTH, D, 2 * FFN_HIDDEN), D ** -0.5),
        ffn_out=nrm((DEPTH, FFN_HIDDEN, D), FFN_HIDDEN ** -0.5),
        ab_in=nrm((N_EVEN, D, AB_IN), D ** -0.5),
        ab_out=nrm((N_EVEN, AB_OUT, D), AB_OUT ** -0.5),
        mla_q_norm=1.0 + nrm((N_EVEN, MLA_Q_LORA), 0.02),
        mla_kv_norm=1.0 + nrm((N_EVEN, MLA_KV_LORA), 0.02),
        mla_wq_b=nrm((N_EVEN, MLA_Q_LORA, MLA_HEADS * (MLA_NOPE + MLA_ROPE)), MLA_Q_LORA ** -0.5),
        mla_wkv_b=nrm((N_EVEN, MLA_KV_LORA, MLA_HEADS * (MLA_NOPE + MLA_V)), MLA_KV_LORA ** -0.5),
        cmlp_v_norm=1.0 + nrm((N_EVEN, CMLP_WIDTH), 0.02),
        cmlp_ws=nrm((N_EVEN, CMLP_GROUPS, CMLP_CHUNK, CMLP_CHUNK), CMLP_CHUNK ** -0.5),
        cmlp_bs=1.0 + nrm((N_EVEN, CMLP_GROUPS, CMLP_CHUNK), 0.02),
        cd_in=nrm((N_ODD, D, CD_IN), D ** -0.5),
        cd_out=nrm((N_ODD, CD_OUT, D), CD_OUT ** -0.5),
        ret_decay_fwd=decay_logit + nrm((N_ODD, RET_HEADS), 0.05),
        ret_decay_bwd=decay_logit + nrm((N_ODD, RET_HEADS), 0.05),
        ret_norm=1.0 + nrm((N_ODD, RET_V_W), 0.02),
        swa_sink=nrm((N_ODD, SWA_Q_HEADS), 0.5),
    )


def reference(x, c, ctx, c_ctx, ada_w, ada_b, norm_mix, norm_ffn, norm_final, ffn_in, ffn_out,
              ab_in, ab_out, mla_q_norm, mla_kv_norm, mla_wq_b, mla_wkv_b, cmlp_v_norm, cmlp_ws, cmlp_bs,
              cd_in, cd_out, ret_decay_fwd, ret_decay_bwd, ret_norm, swa_sink):
    h = ctx
    cond = jax.nn.silu(c)
    cond_ctx = jax.nn.silu(c_ctx)
    for layer in range(DEPTH):
        last = layer == DEPTH - 1
        mx = [m[:, None, :] for m in jnp.split(cond @ ada_w[layer] + ada_b[layer], 6, axis=-1)]
        mh = jnp.split(cond_ctx @ ada_w[layer] + ada_b[layer], 6, axis=-1)
        xn = _modulate(_rms(x, norm_mix[layer]), mx[0], mx[1])
        hn = _modulate(_rms(h, norm_mix[layer]), mh[0], mh[1])
        j = layer // 2
        if layer % 2 == 0:
            yx, yh = _mixer_ab(xn, hn, ab_in[j], ab_out[j], mla_q_norm[j], mla_kv_norm[j], mla_wq_b[j],
                               mla_wkv_b[j], cmlp_v_norm[j], cmlp_ws[j], cmlp_bs[j], not last)
        else:
            yx, yh = _mixer_cd(xn, hn, cd_in[j], cd_out[j], ret_decay_fwd[j], ret_decay_bwd[j],
                               ret_norm[j], swa_sink[j], not last)
        x = x + mx[2] * yx
        x = x + mx[5] * _swiglu(_modulate(_rms(x, norm_ffn[layer]), mx[3], mx[4]), ffn_in[layer], ffn_out[layer])
        if not last:
            h = h + mh[2] * yh
            h = h + mh[5] * _swiglu(_modulate(_rms(h, norm_ffn[layer]), mh[3], mh[4]), ffn_in[layer], ffn_out[layer])
    return _rms(x, norm_final)
```

```python
import numpy as np
import concourse.bass as bass
import concourse.mybir as mybir
from concourse.bass_utils import run_bass_kernel_spmd

F32 = mybir.dt.float32
BF16 = mybir.dt.bfloat16
AF = mybir.ActivationFunctionType
ALU = mybir.AluOpType

ENGS = ("pe", "act", "dve", "pool", "sp")
SEM_CAP = 30000
import os as _os
STRICT = bool(_os.environ.get("KSTRICT"))


class Ins:
    __slots__ = ("eng", "fn", "deps", "signal", "dsem", "ticket", "idx", "semh")

    def __init__(self, eng, fn, dsem):
        self.eng = eng
        self.fn = fn
        self.deps = []
        self.signal = False
        self.dsem = dsem
        self.ticket = None
        self.semh = None
        self.idx = -1


class Prog:
    def __init__(self, nc):
        self.nc = nc
        self.q = {e: [] for e in ENGS}
        self.res = {}
        self.dma_last = {}

    def ins(self, eng, fn, r=(), w=(), dsem=None, cont=False):
        I = Ins(eng, fn, dsem)
        I.idx = len(self.q[eng])
        self.q[eng].append(I)
        if dsem is not None:
            prev = self.dma_last.get(dsem)
            if prev is not None and not cont:
                self._dep(I, prev, True)
            self.dma_last[dsem] = I
        for k in r:
            st = self.res.get(k)
            assert st is not None and st[0] is not None, f"read before write: {k}"
            self._dep(I, st[0], True)
            if isinstance(k, str) and k.startswith("ps"):
                for rd in st[1]:
                    if rd.eng != I.eng:
                        self._dep(I, rd, True)
            st[1].append(I)
        for k in w:
            st = self.res.get(k)
            if st is None:
                st = [None, []]
                self.res[k] = st
            if st[0] is not None and not cont:
                self._dep(I, st[0], False)
            lastc = {}
            for rd in st[1]:
                if rd.dsem is not None:
                    self._dep(I, rd, False)
                elif rd.eng not in lastc or lastc[rd.eng].idx < rd.idx:
                    lastc[rd.eng] = rd
            for rd in lastc.values():
                self._dep(I, rd, False)
            st[0] = I
            st[1] = []
        return I

    def _dep(self, I, D, raw):
        if D is I:
            return
        if D.eng == I.eng and D.dsem is None and I.dsem is None and not STRICT:
            if not raw or I.idx - D.idx > 4:
                return
        D.signal = True
        I.deps.append(D)

    def barrier(self):
        lasts = []
        for e in ENGS:
            if self.q[e]:
                for I in reversed(self.q[e]):
                    if I.dsem is None and I.fn is not None:
                        lasts.append(I)
                        break
        lasts += [I for I in self.dma_last.values()]
        for e in ENGS:
            W = Ins(e, None, None)
            W.idx = len(self.q[e])
            self.q[e].append(W)
            for D in lasts:
                if D.eng == e and D.dsem is None:
                    continue
                D.signal = True
                W.deps.append(D)

    def emit(self):
        nc = self.nc
        esems = {}
        for e in ENGS:
            cnt = 0
            ep = 0
            for I in self.q[e]:
                if I.dsem is None and I.signal:
                    if cnt >= SEM_CAP:
                        ep += 1
                        cnt = 0
                    cnt += 1
                    key = (e, ep)
                    if key not in esems:
                        esems[key] = nc.alloc_semaphore(f"s_{e}_{ep}")
                    I.semh = esems[key]
                    I.ticket = cnt
        dsems = {}
        dcnt = {}
        for e in ENGS:
            for I in self.q[e]:
                if I.dsem is not None:
                    if I.dsem not in dsems:
                        dsems[I.dsem] = nc.alloc_semaphore(f"d_{I.dsem}")
                        dcnt[I.dsem] = 0
                    dcnt[I.dsem] += 16
                    I.semh = dsems[I.dsem]
                    I.ticket = dcnt[I.dsem]
        q = self.q

        def replay(ename, eng):
            waited = {}
            for I in q[ename]:
                need = {}
                for D in I.deps:
                    k = D.semh.num
                    if waited.get(k, 0) >= D.ticket:
                        continue
                    if need.get(k, (None, 0))[1] < D.ticket:
                        need[k] = (D.semh, D.ticket)
                for k, (h, t) in need.items():
                    eng.wait_ge(h, t)
                    waited[k] = t
                if I.fn is None:
                    continue
                bi = I.fn(eng)
                if I.dsem is not None:
                    bi.then_inc(I.semh, 16)
                elif I.signal:
                    bi.then_inc(I.semh, 1)

        with nc.Block() as block:
            @block.tensor
            def _(e):
                replay("pe", e)

            @block.scalar
            def _(e):
                replay("act", e)

            @block.vector
            def _(e):
                replay("dve", e)

            @block.gpsimd
            def _(e):
                replay("pool", e)

            @block.sync
            def _(e):
                replay("sp", e)


D = 1024
CT = 256
FH = 2816
EPS = 1e-6
L_EVEN_IN = 1600
L_ODD_IN = 2304


class Cfg:
    def __init__(self, S=2048, NB=4, DEPTH=4):
        self.S = S
        self.NB = NB
        self.DEPTH = DEPTH
        self.T = S + CT
        self.NE = (DEPTH + 1) // 2
        self.NO = DEPTH // 2
        self.tiles = [(j * 512, 512, True) for j in range(S // 512)] + [(S, CT, False)]


class Arena:
    def __init__(self, nc, base, top):
        self.nc = nc
        self.base = (base + 63) // 64 * 64
        self.top = top
        self.cur = self.base
        self.n = 0

    def alloc(self, shape, dt):
        nb = int(np.prod(shape[1:])) * (4 if dt == F32 else 2)
        nb = (nb + 63) // 64 * 64
        off = self.cur
        assert off + nb <= self.top, f"arena overflow {off + nb - self.top}"
        self.cur += nb
        self.n += 1
        return self.nc.alloc_sbuf_tensor_at(f"ar{self.n}", list(shape), dt, offset=off)

    def mark(self):
        return self.cur

    def release(self, m):
        self.cur = m


class Builder:
    def __init__(self, cfg):
        self.cfg = cfg
        nc = bass.Bass("TRN2", target_bir_lowering=False)
        self.nc = nc
        self.p = Prog(nc)
        self.uid = 0
        c = cfg
        NB, S, T, L = c.NB, c.S, c.T, c.DEPTH
        di = lambda n, sh: nc.dram_tensor(n, list(sh), F32, kind="ExternalInput").ap()
        self.d = dict(
            xT=di("xT", [NB, D, S]), ctxT=di("ctxT", [NB, D, CT]), cT=di("cT", [D, NB + 1]),
            ada_w=di("ada_w", [L, D, 6 * D]), ada_bT=di("ada_bT", [128, L * 48]),
            nmixT=di("nmixT", [128, L * 8]), nffnT=di("nffnT", [128, L * 8]), nfinT=di("nfinT", [128, 8]),
            ffn_in=di("ffn_in", [L, D, 2 * FH]), ffn_out=di("ffn_out", [L, FH, D]),
            ab_in=di("ab_in", [c.NE, D, 1600]), ab_out=di("ab_out", [c.NE, D, D]),
            qnT=di("qnT", [128, c.NE * 2]), kvnT=di("kvnT", [128, c.NE * 2]),
            wq_b=di("wq_b", [c.NE, 256, 768]), wkv_b=di("wkv_b", [c.NE, 256, 1024]),
            vnorm_bc=di("vnorm_bc", [128, c.NE * 512]), wsT=di("wsT", [c.NE, 128, 4 * 128]),
            bs_row=di("bs_row", [1, c.NE * 512]),
            cd_in=di("cd_in", [max(c.NO, 1), D, 2304]), cd_out=di("cd_out", [max(c.NO, 1), D, D]),
            dec_bc=di("dec_bc", [128, max(c.NO, 1) * 8]), retnT=di("retnT", [128, max(c.NO, 1) * 4]),
            sink_row=di("sink_row", [1, max(c.NO, 1) * 1024]),
            ropeC=di("ropeC", [64, S]), ropeS=di("ropeS", [64, S]),
            cst=di("cst", [128, 2048]),
        )
        self.outT = nc.dram_tensor("outT", [NB, D, S], F32, kind="ExternalOutput").ap()
        al = nc.alloc_sbuf_tensor
        self.XH = al("XH", [128, 8, T], F32)
        self.MODS = al("MODS", [128, L, 6, 8, NB + 1], F32)
        self.AM = al("AM", [128, L, 2, 8, NB + 1], F32)
        self.NMIX = al("NMIX", [128, L * 8], F32)
        self.NFFN = al("NFFN", [128, L * 8], F32)
        self.NFIN = al("NFIN", [128, 8], F32)
        self.ADAB = al("ADAB", [128, L * 48], F32)
        self.CST = al("CST", [128, 2048], F32)
        self.CSTB = al("CSTB", [128, 512], BF16)
        self.ROPC = al("ROPC", [64, S], BF16)
        self.ROPS = al("ROPS", [64, S], BF16)
        self.EPSC = al("EPSC", [128, 1], F32)
        self.ps = {}
        self.psi = {}
        for pool, n in (("mm", 3), ("acc", 2), ("den", 2)):
            self.ps[pool] = [nc.alloc_psum_tensor(f"ps_{pool}{i}", [128, 512], F32) for i in range(n)]
            self.psi[pool] = 0
        self.pst = nc.alloc_psum_tensor("ps_tr", [128, 1024], BF16)
        self.ar = Arena(nc, nc.sbuf_base + 64, nc.sbuf_top)

    def k(self, s):
        self.uid += 1
        return f"{s}#{self.uid}"

    def psum(self, pool):
        i = self.psi[pool]
        self.psi[pool] = (i + 1) % len(self.ps[pool])
        return self.ps[pool][i], f"ps_{pool}{i}"

    def I(self, eng, fn, r=(), w=(), dsem=None, cont=False):
        return self.p.ins(eng, fn, r, w, dsem, cont)

    def load(self, q, dst_ap, src_ap, key, dsem, keys=None):
        w = keys if keys is not None else [key]
        if len(dst_ap.shape) == 3 and dst_ap.shape[1] > 1:
            for i in range(dst_ap.shape[1]):
                self.I(q, (lambda i: lambda e: e.dma_start(out=dst_ap[:, i, :], in_=src_ap[:, i, :]))(i), w=w, dsem=dsem, cont=(i > 0))
        else:
            self.I(q, lambda e: e.dma_start(out=dst_ap, in_=src_ap), w=w, dsem=dsem)

    def mm(self, out, lhsT, rhs, start, stop, r, w):
        self.I("pe", lambda e: e.matmul(out, lhsT, rhs, start=start, stop=stop), r=r, w=w)

    def act(self, out, in_, func, r, w, scale=1.0, bias=None, accum=None):
        kw = dict(out=out, in_=in_, func=func, scale=scale)
        if bias is not None:
            kw["bias"] = bias
        if accum is not None:
            kw["accum_out"] = accum
        self.I("act", lambda e: e.activation(**kw), r=r, w=w)

    def tt(self, eng, out, in0, in1, op, r, w):
        self.I(eng, lambda e: e.tensor_tensor(out=out, in0=in0, in1=in1, op=op), r=r, w=w)

    def stt(self, eng, out, in0, scalar, in1, op0, op1, r, w):
        self.I(eng, lambda e: e.scalar_tensor_tensor(out=out, in0=in0, scalar=scalar, in1=in1, op0=op0, op1=op1), r=r, w=w)

    def recip(self, out, in_, r, w):
        self.I("dve", lambda e: e.reciprocal(out=out, in_=in_), r=r, w=w)

    def rstd(self, ps_ap, out_ap, inv_n, r, w):
        self.act(out_ap, ps_ap, AF.Sqrt, r=r + ["EPSC"], w=w, scale=inv_n, bias=self.EPSC[0:out_ap.shape[0], 0:1])
        self.recip(out_ap, out_ap, r=w, w=w)

    def ones(self, kp, m):
        return self.CSTB[0:kp, 0:m]

    def ident(self, n):
        return self.CSTB[0:n, 128:128 + n]

    def RT(self):
        return self.CSTB[0:64, 256:320]

    def setup(self):
        c = self.cfg
        d = self.d
        self.load("sp", self.CST[:], d["cst"], "CST", "cst")
        self.I("pool", lambda e: e.dma_start(out=self.CSTB[:, 0:320], in_=d["cst"][:, 0:320]), w=["CSTB"], dsem="cstb")
        self.I("pool", lambda e: e.dma_start(out=self.ROPC[:], in_=d["ropeC"]), w=["ROPC"], dsem="ropc")
        self.I("pool", lambda e: e.dma_start(out=self.ROPS[:], in_=d["ropeS"]), w=["ROPS"], dsem="rops")
        self.load("sp", self.NMIX[:], d["nmixT"], "NMIX", "nmix")
        self.load("sp", self.NFFN[:], d["nffnT"], "NFFN", "nffn")
        self.load("sp", self.NFIN[:], d["nfinT"], "NFIN", "nfin")
        self.load("sp", self.ADAB[:], d["ada_bT"], "ADAB", "adab")
        self.I("dve", lambda e: e.memset(self.EPSC[:], EPS), w=["EPSC"])
        NB1 = c.NB + 1
        m = self.ar.mark()
        cT = self.ar.alloc([128, 8, NB1], F32)
        cB = self.ar.alloc([128, 8, NB1], BF16)
        self.load("sp", cT[:], d["cT"].rearrange("(c p) n -> p c n", p=128), "cT", "ct")
        self.act(cB[:], cT[:], AF.Silu, r=["cT"], w=["cB"])
        slabs = [self.ar.alloc([128, 8, 1024], BF16) for _ in range(2)]
        n = 0
        for l in range(c.DEPTH):
            for wv in range(6):
                sl = slabs[n % 2]
                sk = f"adaslab{n % 2}"
                src = d["ada_w"][l].rearrange("(kc p) n -> p kc n", p=128)[:, :, wv * 1024:(wv + 1) * 1024]
                self.load("pool", sl[:], src, sk, sk)
                n += 1
                for oc in range(8):
                    ps, pk = self.psum("mm")
                    for kc in range(8):
                        self.mm(ps[:, 0:NB1], sl[:, kc, oc * 128:(oc + 1) * 128], cB[:, kc, :], kc == 0, kc == 7, r=[sk, "cB"], w=[pk])
                    col = l * 48 + wv * 8 + oc
                    self.act(self.MODS[:, l, wv, oc, :], ps[:, 0:NB1], AF.Identity, r=[pk, "ADAB"], w=["MODS"], bias=self.ADAB[:, col:col + 1])
        for l in range(c.DEPTH):
            for which, (nt, nk, wsc) in enumerate(((self.NMIX, "NMIX", 1), (self.NFFN, "NFFN", 4))):
                for oc in range(8):
                    col = l * 8 + oc
                    self.I("dve", (lambda l, which, oc, nt, col, wsc: lambda e: e.tensor_scalar(
                        out=self.AM[:, l, which, oc, :], in0=self.MODS[:, l, wsc, oc, :], scalar1=1.0, scalar2=nt[:, col:col + 1],
                        op0=ALU.add, op1=ALU.mult))(l, which, oc, nt, col, wsc), r=["MODS", nk], w=["AM"])
        self.p.barrier()
        self.ar.release(m)

    def norm_tile(self, l, which, b, off, n, is_x, dst, dst_off, dkey, tmp):
        col = b if is_x else self.cfg.NB
        sq, rs, tf = tmp["sq"], tmp["rs"], tmp["tf"]
        pd, pk = self.psum("den")
        for cc in range(8):
            s = sq[cc % 2]
            sk = f"nsq{cc % 2}"
            self.act(s[:, 0:n], self.XH[:, cc, off:off + n], AF.Square, r=[("XH", cc, off)], w=[sk])
            self.mm(pd[:, 0:n], self.ones(128, 128), s[:, 0:n], cc == 0, cc == 7, r=[sk, "CSTB"], w=[pk])
        self.rstd(pd[:, 0:n], rs[:, 0:n], 1.0 / D, r=[pk], w=["nrs"])
        wsh = 0 if which == 0 else 3
        for cc in range(8):
            t = tf[cc % 2]
            tk = f"ntf{cc % 2}"
            self.stt("dve", t[:, 0:n], self.XH[:, cc, off:off + n], self.AM[:, l, which, cc, col:col + 1], rs[:, 0:n],
                     ALU.mult, ALU.mult, r=[("XH", cc, off), "AM", "nrs"], w=[tk])
            self.act(dst[:, cc, dst_off:dst_off + n], t[:, 0:n], AF.Identity, r=[tk, "MODS"], w=[(dkey, cc, dst_off)],
                     bias=self.MODS[:, l, wsh, cc, col:col + 1])

    def norm_tmp(self):
        return dict(sq=[self.ar.alloc([128, 512], BF16) for _ in range(2)], rs=self.ar.alloc([128, 512], F32),
                    tf=[self.ar.alloc([128, 512], F32) for _ in range(2)])

    def resid(self, ps, pk, l, wg, b, cc, off, n, is_x, extra_r=()):
        col = b if is_x else self.cfg.NB
        self.stt("dve", self.XH[:, cc, off:off + n], ps, self.MODS[:, l, wg, cc, col:col + 1], self.XH[:, cc, off:off + n],
                 ALU.mult, ALU.add, r=[pk, "MODS", ("XH", cc, off)] + list(extra_r), w=[("XH", cc, off)])

    def rope(self, ps_ap, pk, off, n, dst_ap, dkey, tmp, scale=1.0, is_x=True):
        v3 = (lambda a: a.rearrange("p (a q) -> p a q", q=128)) if len(dst_ap.shape) == 3 else (lambda a: a)
        if not is_x or _os.environ.get("KROPECOPY"):
            self.act(dst_ap, v3(ps_ap), AF.Copy, r=[pk], w=[dkey], scale=scale)
            return
        i = tmp["i"] = (tmp.get("i", 0) + 1) % 2
        zb, t1, t2 = tmp["zb"][i], tmp["t1"][i], tmp["t2"][i]
        lvl = int(_os.environ.get("KROPELVL", "9")) if tmp.get("dbg") else 9
        self.act(zb[:, 0:n], ps_ap, AF.Copy, r=[pk], w=[f"rzb{i}"], scale=scale)
        if lvl >= 2:
            var = _os.environ.get("KSTTVAR", "") if tmp.get("dbg") else ""
            in0 = zb[:, 0:n] if var == "in0" else ps_ap
            in1 = zb[:, 0:n] if var == "in1" else self.ROPC[:, off:off + n]
            self.stt("dve", t1[:, 0:n], in0, scale, in1, ALU.mult, ALU.mult, r=[pk, "ROPC", f"rzb{i}"], w=[f"rt1{i}"])
        if lvl >= 3:
            pr, prk = self.psum("den")
            self.mm(pr[0:64, 0:n], self.RT(), zb[:, 0:n], True, True, r=[f"rzb{i}", "CSTB"], w=[prk])
        if lvl >= 4:
            self.tt("dve", t2[:, 0:n], pr[0:64, 0:n], self.ROPS[:, off:off + n], ALU.mult, r=[prk, "ROPS"], w=[f"rt2{i}"])
        if lvl < 5:
            self.act(dst_ap, v3(ps_ap), AF.Copy, r=[pk], w=[dkey], scale=scale)
            return
        self.tt("dve" if _os.environ.get("KDVEADD") else "pool", dst_ap, v3(t1[:, 0:n]), v3(t2[:, 0:n]), ALU.add, r=[f"rt1{i}", f"rt2{i}"], w=[dkey])

    def rope_tmp(self):
        return dict(zb=[self.ar.alloc([64, 512], BF16) for _ in range(2)], t1=[self.ar.alloc([64, 512], F32) for _ in range(2)],
                    t2=[self.ar.alloc([64, 512], F32) for _ in range(2)])

    def ffn(self, l, b):
        c = self.cfg
        d = self.d
        T = c.T
        self.p.barrier()
        m = self.ar.mark()
        ntmp = self.norm_tmp()
        xn = self.ar.alloc([128, 8, T], BF16)
        xk = self.k("fxn")
        ftiles = [t for t in c.tiles if t[2] or l < c.DEPTH - 1]
        for (off, n, is_x) in ftiles:
            self.norm_tile(l, 1, b, off, n, is_x, xn, off, xk, ntmp)
        g = [self.ar.alloc([128, 2, T], BF16) for _ in range(2)]
        win = [self.ar.alloc([128, 8, 512], BF16) for _ in range(2)]
        wout = [self.ar.alloc([128, 2, 1024], BF16) for _ in range(2)]
        sa = [self.ar.alloc([128, 512], BF16) for _ in range(2)]
        fin = d["ffn_in"][l].rearrange("(kc p) n -> p kc n", p=128)
        NG = FH // 256
        for grp in range(NG):
            s = grp % 2
            j0 = grp * 256
            self.load("pool", win[s][:, :, 0:256], fin[:, :, j0:j0 + 256], f"fwa{s}", f"fwa{s}")
            self.load("pool", win[s][:, :, 256:512], fin[:, :, FH + j0:FH + j0 + 256], f"fwb{s}", f"fwb{s}")
            self.load("pool", wout[s][:], d["ffn_out"][l][j0:j0 + 256, :].rearrange("(blk p) n -> p blk n", p=128), f"fwo{s}", f"fwo{s}")
            gk = f"fg{s}"
            for (off, n, is_x) in ftiles:
                for blk in range(2):
                    pa, pak = self.psum("mm")
                    pb, pbk = self.psum("mm")
                    for kc in range(8):
                        self.mm(pa[:, 0:n], win[s][:, kc, blk * 128:(blk + 1) * 128], xn[:, kc, off:off + n], kc == 0, kc == 7, r=[f"fwa{s}", (xk, kc, off)], w=[pak])
                    for kc in range(8):
                        self.mm(pb[:, 0:n], win[s][:, kc, 256 + blk * 128:256 + (blk + 1) * 128], xn[:, kc, off:off + n], kc == 0, kc == 7, r=[f"fwb{s}", (xk, kc, off)], w=[pbk])
                    si = (blk) % 2
                    self.act(sa[si][:, 0:n], pa[:, 0:n], AF.Silu, r=[pak], w=[f"fsa{si}"])
                    self.tt("dve", g[s][:, blk, off:off + n], pb[:, 0:n], sa[si][:, 0:n], ALU.mult, r=[pbk, f"fsa{si}"], w=[(gk, blk, off)])
            for (off, n, is_x) in ftiles:
                for cc in range(8):
                    po, pok = self.psum("acc")
                    for blk in range(2):
                        self.mm(po[:, 0:n], wout[s][:, blk, cc * 128:(cc + 1) * 128], g[s][:, blk, off:off + n], blk == 0, blk == 1, r=[f"fwo{s}", (gk, blk, off)], w=[pok])
                    self.resid(po[:, 0:n], pok, l, 5, b, cc, off, n, is_x)
        self.p.barrier()
        self.ar.release(m)

    def proj_fm(self, W, wkey, c0, M, X, xkey, xoff, n, nk=8, pool="mm"):
        ps, pk = self.psum(pool)
        for kc in range(nk):
            self.mm(ps[0:M, 0:n], W[:, kc, c0:c0 + M], X[:, kc, xoff:xoff + n], kc == 0, kc == nk - 1, r=[wkey, (xkey, kc, xoff)], w=[pk])
        return ps, pk

    def even_mixer(self, l, b):
        c = self.cfg
        d = self.d
        j = l // 2
        T, S = c.T, c.S
        last = l == c.DEPTH - 1
        self.p.barrier()
        m0 = self.ar.mark()
        kvn = self.ar.alloc([128, 2, T], BF16)
        qn = self.ar.alloc([128, 2, T], BF16)
        kpe = self.ar.alloc([64, T], BF16)
        kvk, qnk, kpk = self.k("kvn"), self.k("qn"), self.k("kpe")
        m1 = self.ar.mark()
        ntmp = self.norm_tmp()
        rtmp = self.rope_tmp()
        Win = self.ar.alloc([128, 8, 1600], BF16)
        Wo2 = self.ar.alloc([128, 4, 1024], BF16)
        self.load("pool", Win[:], d["ab_in"][j].rearrange("(kc p) n -> p kc n", p=128), "eWin", "eWin")
        self.load("pool", Wo2[:], d["ab_out"][j][512:1024, :].rearrange("(kc p) n -> p kc n", p=128), "eWo2", "eWo2")
        vnb = self.ar.alloc([128, 512], F32)
        self.load("sp", vnb[:], d["vnorm_bc"][:, j * 512:(j + 1) * 512], "evnb", "evnb")
        wsT = self.ar.alloc([128, 512], BF16)
        self.load("pool", wsT[:], d["wsT"][j], "ewsT", "ewsT")
        bsr = self.ar.alloc([1, 512], BF16)
        self.load("pool", bsr[:], d["bs_row"][:, j * 512:(j + 1) * 512], "ebsr", "ebsr")
        lnv = self.ar.alloc([128, 4], F32)
        self.load("sp", lnv[:, 0:2], d["kvnT"][:, j * 2:j * 2 + 2], "elnv0", "elnv0")
        self.load("sp", lnv[:, 2:4], d["qnT"][:, j * 2:j * 2 + 2], "elnv1", "elnv1")
        xn = self.ar.alloc([128, 8, 512], BF16)
        zf = self.ar.alloc([128, 2, 512], F32)
        sqb = self.ar.alloc([128, 2, 512], BF16)
        rs = self.ar.alloc([128, 512], F32)
        u = self.ar.alloc([128, 4, 512], BF16)
        vg = self.ar.alloc([128, 512], F32)
        sqv = self.ar.alloc([128, 512], F32)
        ss = self.ar.alloc([128, 4], F32)
        vtok = self.ar.alloc([128, 4, 512], BF16)
        cm = self.ar.alloc([128, 4, 512], BF16)
        xk = self.k("exn")
        tiles = [t for t in c.tiles]
        for (off, n, is_x) in tiles:
            self.norm_tile(l, 0, b, off, n, is_x, xn, 0, xk, ntmp)

            def latent(c0, nv0, dst, dkey):
                for blk in range(2):
                    ps, pk = self.proj_fm(Win, "eWin", c0 + blk * 128, 128, xn, xk, 0, n)
                    self.act(zf[:, blk, 0:n], ps[:, 0:n], AF.Copy, r=[pk], w=[("ezf", blk)])
                    self.act(sqb[:, blk, 0:n], ps[:, 0:n], AF.Square, r=[pk], w=[("esq", blk)])
                pd, pdk = self.psum("den")
                for blk in range(2):
                    self.mm(pd[:, 0:n], self.ones(128, 128), sqb[:, blk, 0:n], blk == 0, blk == 1, r=[("esq", blk), "CSTB"], w=[pdk])
                self.rstd(pd[:, 0:n], rs[:, 0:n], 1.0 / 256, r=[pdk], w=["ers"])
                for blk in range(2):
                    self.stt("dve", dst[:, blk, off:off + n], zf[:, blk, 0:n], lnv[:, nv0 + blk:nv0 + blk + 1], rs[:, 0:n], ALU.mult, ALU.mult,
                             r=[("ezf", blk), "elnv0", "elnv1", "ers"], w=[(dkey, blk, off)])

            latent(0, 0, kvn, kvk)
            latent(320, 2, qn, qnk)
            ps, pk = self.proj_fm(Win, "eWin", 256, 64, xn, xk, 0, n)
            self.rope(ps[0:64, 0:n], pk, off, n, kpe[:, off:off + n], (kpk, off), rtmp, 1.0, is_x)
            for g in range(4):
                ps, pk = self.proj_fm(Win, "eWin", 576 + g * 128, 128, xn, xk, 0, n)
                self.act(u[:, g, 0:n], ps[:, 0:n], AF.Gelu, r=[pk], w=[("eu", g)])
            nch = n // 128
            for ch in range(nch):
                ps, pk = self.psum("mm")
                for kc in range(8):
                    self.mm(ps[:, 0:512], xn[:, kc, ch * 128:(ch + 1) * 128], Win[:, kc, 1088:1600], kc == 0, kc == 7, r=["eWin", (xk, kc, 0)], w=[pk])
                self.act(vg[:], ps[:, 0:512], AF.Gelu, r=[pk], w=["evg"])
                self.act(sqv[:], vg[:], AF.Square, r=["evg"], w=["esqv"])
                self.I("dve", lambda e: e.tensor_reduce(out=ss[:], in_=sqv[:].rearrange("p (g dd) -> p g dd", g=4), axis=mybir.AxisListType.X, op=ALU.add), r=["esqv"], w=["ess"])
                self.rstd(ss[:], ss[:], 1.0 / 128, r=["ess"], w=["ess"])
                for g in range(4):
                    self.stt("dve", vtok[:, ch, g * 128:(g + 1) * 128], vg[:, g * 128:(g + 1) * 128], ss[:, g:g + 1], vnb[:, g * 128:(g + 1) * 128],
                             ALU.mult, ALU.mult, r=["evg", "ess", "evnb"], w=[("evt", ch)])
            for g in range(4):
                ps, pk = self.psum("mm")
                for ch in range(nch):
                    self.mm(ps[:, ch * 128:(ch + 1) * 128], vtok[:, ch, g * 128:(g + 1) * 128], wsT[:, g * 128:(g + 1) * 128], True, False, r=[("evt", ch), "ewsT"], w=[pk])
                    self.mm(ps[:, ch * 128:(ch + 1) * 128], self.ones(1, 128), bsr[0:1, g * 128:(g + 1) * 128], False, True, r=["CSTB", "ebsr"], w=[pk])
                self.tt("dve", cm[:, g, 0:n], ps[:, 0:n], u[:, g, 0:n], ALU.mult, r=[pk, ("eu", g)], w=[("ecm", g)])
            for cc in range(8):
                po, pok = self.psum("acc")
                for g in range(4):
                    self.mm(po[:, 0:n], Wo2[:, g, cc * 128:(cc + 1) * 128], cm[:, g, 0:n], g == 0, g == 3, r=["eWo2", ("ecm", g)], w=[pok])
                self.resid(po[:, 0:n], pok, l, 2, b, cc, off, n, is_x)
        self.p.barrier()
        self.ar.release(m1)
        import os
        if os.environ.get("KSTOP") == "A":
            self.ar.release(m0)
            return
        rtmpB = self.rope_tmp()
        wkv = self.ar.alloc([128, 2, 1024], BF16)
        wv = self.ar.alloc([128, 2, 4, 128], BF16)
        self.load("pool", wkv[:], d["wkv_b"][j].rearrange("(kc p) n -> p kc n", p=128), "ewkv", "ewkv")
        for kc in range(2):
            self.load("pool", wv[:, kc, :, :], d["wkv_b"][j][kc * 128:(kc + 1) * 128, :].rearrange("p (h two dd) -> p h two dd", two=2, dd=128)[:, :, 1, :], f"ewv{kc}", f"ewv{kc}")
        Kn = self.ar.alloc([128, 4, T], BF16)
        Vt = self.ar.alloc([128, T // 128, 512], BF16)
        Knk, Vtk = self.k("Kn"), self.k("Vt")
        for (off, n, is_x) in c.tiles:
            for h in range(4):
                ps, pk = self.proj_fm(wkv, "ewkv", h * 256, 128, kvn, kvk, off, n, nk=2)
                self.act(Kn[:, h, off:off + n], ps[:, 0:n], AF.Copy, r=[pk], w=[(Knk, h, off)])
            for ch in range(n // 128):
                gch = off // 128 + ch
                ps, pk = self.psum("mm")
                for kc in range(2):
                    self.mm(ps[:, 0:512], kvn[:, kc, gch * 128:(gch + 1) * 128], wv[:, kc, :, :], kc == 0, kc == 1, r=[f"ewv{kc}", (kvk, kc, off)], w=[pk])
                self.act(Vt[:, gch, :], ps[:, 0:512], AF.Copy, r=[pk], w=[(Vtk, gch)])
        if os.environ.get("KSTOP") == "K":
            self.p.barrier()
            self.ar.release(m0)
            return
        wq = self.ar.alloc([128, 2, 768], BF16)
        Wo1 = self.ar.alloc([128, 4, 1024], BF16)
        self.load("pool", wq[:], d["wq_b"][j].rearrange("(kc p) n -> p kc n", p=128), "ewq", "ewq")
        self.load("pool", Wo1[:], d["ab_out"][j][0:512, :].rearrange("(kc p) n -> p kc n", p=128), "eWo1", "eWo1")
        rtmp = rtmpB
        rtmp["dbg"] = True
        Qn = self.ar.alloc([128, 4, 512], BF16)
        Qp = self.ar.alloc([64, 4, 512], BF16)
        o = self.ar.alloc([128, 4, 512], BF16)
        Pt = [self.ar.alloc([128, 512], BF16) for _ in range(3)]
        rd = self.ar.alloc([128, 512], F32)
        scale = 192.0 ** -0.5
        pi = 0
        for (off, n, is_x) in c.tiles:
            if os.environ.get("KSTOP") == "B0":
                continue
            if is_x:
                kchunks = list(range(T // 128))
            else:
                kchunks = list(range(S // 128, T // 128))
            for h in range(4):
                ps, pk = self.proj_fm(wq, "ewq", h * 192, 128, qn, qnk, off, n, nk=2)
                self.act(Qn[:, h, 0:n], ps[:, 0:n], AF.Copy, r=[pk], w=[("eQn", h)])
                if os.environ.get("KNOROPE"):
                    continue
                if os.environ.get("KH0") and h > 0:
                    continue
                for _ in range(int(os.environ.get("KSHIFT", "0"))):
                    self.psum("mm")
                ps, pk = self.proj_fm(wq, "ewq", h * 192 + 128, 64, qn, qnk, off, n, nk=2)
                self.rope(ps[0:64, 0:n], pk, off, n, Qp[:, h, 0:n], ("eQp", h), rtmp, 1.0, is_x)
            if os.environ.get("KSTOP") == "B1":
                continue
            for h in range(4):
                pO, pOk = self.psum("acc")
                pD, pDk = self.psum("den")

                def qk(kc):
                    ps, pk = self.psum("mm")
                    koff = (kc * 128) // 512 * 512 if kc * 128 < S else S
                    b2 = os.environ.get("KSTOP") == "B2"
                    self.mm(ps[:, 0:n], Kn[:, h, kc * 128:(kc + 1) * 128], Qn[:, h, 0:n], True, b2, r=[(Knk, h, koff), ("eQn", h)], w=[pk])
                    if not b2:
                        self.mm(ps[:, 0:n], kpe[:, kc * 128:(kc + 1) * 128], Qp[:, h, 0:n], False, True, r=[(kpk, koff), ("eQp", h)], w=[pk])
                    return ps, pk

                nxt = qk(kchunks[0])
                for i, kc in enumerate(kchunks):
                    ps, pk = nxt
                    if i + 1 < len(kchunks):
                        nxt = qk(kchunks[i + 1])
                    P = Pt[pi % 3]
                    Pk = f"ePt{pi % 3}"
                    pi += 1
                    self.act(P[:, 0:n], ps[:, 0:n], AF.Exp, r=[pk], w=[Pk], scale=scale)
                    self.mm(pO[:, 0:n], Vt[:, kc, h * 128:(h + 1) * 128], P[:, 0:n], i == 0, i == len(kchunks) - 1, r=[(Vtk, kc), Pk], w=[pOk])
                    self.mm(pD[:, 0:n], self.ones(128, 128), P[:, 0:n], i == 0, i == len(kchunks) - 1, r=["CSTB", Pk], w=[pDk])
                self.recip(rd[:, 0:n], pD[:, 0:n], r=[pDk], w=["erd"])
                self.tt("dve", o[:, h, 0:n], pO[:, 0:n], rd[:, 0:n], ALU.mult, r=[pOk, "erd"], w=[("eo", h)])
            for cc in range(8):
                po, pok = self.psum("acc")
                for h in range(4):
                    self.mm(po[:, 0:n], Wo1[:, h, cc * 128:(cc + 1) * 128], o[:, h, 0:n], h == 0, h == 3, r=["eWo1", ("eo", h)], w=[pok])
                self.resid(po[:, 0:n], pok, l, 2, b, cc, off, n, is_x)
        self.p.barrier()
        self.ar.release(m0)

    def odd_mixer(self, l, b):
        c = self.cfg
        d = self.d
        j = l // 2
        T, S = c.T, c.S
        last = l == c.DEPTH - 1
        NCH = T // 128
        XCH = S // 128
        CI = d["cd_in"][j].rearrange("(kc p) n -> p kc n", p=128)
        self.p.barrier()
        m0 = self.ar.mark()
        xn = self.ar.alloc([128, 8, T], BF16)
        xk = self.k("oxn")
        mt = self.ar.mark()
        ntmp = self.norm_tmp()
        for (off, n, is_x) in c.tiles:
            self.norm_tile(l, 0, b, off, n, is_x, xn, off, xk, ntmp)
        self.p.barrier()
        self.ar.release(mt)
        CS = self.CST
        RQK, RKQ = CS[:, 320:448], CS[:, 448:576]
        MGE, MLE = CS[:, 576:1088], CS[:, 1088:1600]
        IO1, IO2 = CS[0:64, 1600:1728], CS[0:64, 1728:1856]
        PC = CS[:, 1856:1858]
        dec = self.ar.alloc([128, 8], F32)
        lg = self.ar.alloc([128, 8], F32)
        self.load("sp", dec[:], d["dec_bc"][:, j * 8:(j + 1) * 8], "odec", "odec")
        self.act(lg[:], dec[:], AF.Exp, r=["odec"], w=["olg"], scale=-1.0)
        self.act(lg[:], lg[:], AF.Ln, r=["olg"], w=["olg"], bias=1.0)
        self.I("dve", lambda e: e.tensor_scalar(out=lg[:], in0=lg[:], scalar1=-1.0, scalar2=None, op0=ALU.mult), r=["olg"], w=["olg"])
        Dc = self.ar.alloc([128, 4, 128], F32)
        e1 = self.ar.alloc([128, 128], F32)
        e2 = self.ar.alloc([128, 128], F32)
        qdf = self.ar.alloc([64, 4, 128], F32)
        qdb = self.ar.alloc([64, 4, 128], F32)
        kdec = self.ar.alloc([128, 8], F32)
        cdec = self.ar.alloc([128, 8], F32)
        rnv = self.ar.alloc([128, 4], F32)
        self.load("sp", rnv[:], d["retnT"][:, j * 4:(j + 1) * 4], "ornv", "ornv")
        sinkr = self.ar.alloc([1, 1024], BF16)
        sinkf = self.ar.alloc([1, 1024], F32)
        self.load("sp", sinkf[:], d["sink_row"][:, j * 1024:(j + 1) * 1024], "osinkf", "osinkf")
        self.act(sinkr[:], sinkf[:], AF.Exp, r=["osinkf"], w=["osinkr"])
        for h in range(4):
            self.act(e1[:], RQK, AF.Exp, r=["CST", "olg"], w=["oe1"], scale=lg[:, h:h + 1])
            self.tt("dve", e1[:], e1[:], MGE[:, 0:128], ALU.mult, r=["oe1", "CST"], w=["oe1"])
            self.act(e2[:], RKQ, AF.Exp, r=["CST", "olg"], w=["oe2"], scale=lg[:, 4 + h:5 + h])
            self.tt("dve", e2[:], e2[:], MLE[:, 0:128], ALU.mult, r=["oe2", "CST"], w=["oe2"])
            self.tt("dve", Dc[:, h, :], e1[:], e2[:], ALU.add, r=["oe1", "oe2"], w=["oDc"])
            self.act(qdf[:, h, :], IO1, AF.Exp, r=["CST", "olg"], w=["oqd"], scale=lg[0:64, h:h + 1])
            self.act(qdb[:, h, :], IO2, AF.Exp, r=["CST", "olg"], w=["oqd"], scale=lg[0:64, 4 + h:5 + h])
            self.act(kdec[:, h:h + 1], PC[:, 0:1], AF.Exp, r=["CST", "olg"], w=["okd"], scale=lg[:, h:h + 1])
            self.act(kdec[:, 4 + h:5 + h], PC[:, 1:2], AF.Exp, r=["CST", "olg"], w=["okd"], scale=lg[:, 4 + h:5 + h])
        self.act(cdec[:], lg[:], AF.Exp, r=["olg"], w=["ocd"], scale=128.0)
        m1 = self.ar.mark()
        rtmp = self.rope_tmp()
        for h in range(4):
            wk = self.ar.alloc([128, 8, 64], BF16)
            wv = self.ar.alloc([128, 8, 128], BF16)
            wq = self.ar.alloc([128, 8, 64], BF16)
            wg = self.ar.alloc([128, 8, 128], BF16)
            wo = self.ar.alloc([128, 1024], BF16)
            self.load("pool", wk[:], CI[:, :, h * 64:(h + 1) * 64], "owk", "owk")
            self.load("pool", wv[:], CI[:, :, 256 + h * 128:256 + (h + 1) * 128], "owv", "owv")
            self.load("pool", wq[:], CI[:, :, 1024 + h * 64:1024 + (h + 1) * 64], "owq", "owq")
            self.load("pool", wg[:], CI[:, :, 1280 + h * 128:1280 + (h + 1) * 128], "owg", "owg")
            self.load("pool", wo[:], d["cd_out"][j][h * 128:(h + 1) * 128, :], "owo", "owo")
            snaps = self.ar.alloc([64, NCH, 128], BF16)
            Sb = self.ar.alloc([64, 128], F32)
            Sf = self.ar.alloc([64, 128], F32)
            Sfb = self.ar.alloc([64, 128], BF16)
            krT = self.ar.alloc([64, 512], BF16)
            qrT = self.ar.alloc([64, 512], BF16)
            qf = self.ar.alloc([64, 512], BF16)
            qb = self.ar.alloc([64, 512], BF16)
            vt = self.ar.alloc([128, 4, 128], BF16)
            kd = self.ar.alloc([128, 4, 64], BF16)
            sg = self.ar.alloc([128, 512], BF16)
            At = [self.ar.alloc([128, 128], BF16) for _ in range(2)]
            yf = self.ar.alloc([128, 512], F32)
            ysq = self.ar.alloc([128, 512], BF16)
            rs = self.ar.alloc([128, 512], F32)
            yg = self.ar.alloc([128, 512], BF16)
            snk = self.k("snap")

            def kv_tile(off, n, is_x, dirn):
                ps, pk = self.proj_fm(wk, "owk", 0, 64, xn, xk, off, n)
                self.rope(ps[0:64, 0:n], pk, off, n, krT[:, 0:n], "okrT", rtmp, 0.125, is_x)
                for ch in range(n // 128):
                    ps, pk = self.psum("mm")
                    for kc in range(8):
                        self.mm(ps[:, 0:128], xn[:, kc, off + ch * 128:off + (ch + 1) * 128], wv[:, kc, :], kc == 0, kc == 7, r=["owv", (xk, kc, off)], w=[pk])
                    self.act(vt[:, ch, :], ps[:, 0:128], AF.Copy, r=[pk], w=[("ovt", ch)])
                    self.I("pe", (lambda ch, krT: lambda e: e.transpose(self.pst[:, ch * 64:(ch + 1) * 64], krT[:, ch * 128:(ch + 1) * 128], self.ident(64)))(ch, krT),
                           r=["okrT", "CSTB"], w=["pst"])
                    col = h if dirn == 0 else 4 + h
                    self.I("dve", (lambda ch, col, kd, kdec: lambda e: e.tensor_scalar(out=kd[:, ch, :], in0=self.pst[:, ch * 64:(ch + 1) * 64], scalar1=kdec[:, col:col + 1],
                                                                                         scalar2=None, op0=ALU.mult))(ch, col, kd, kdec), r=["pst", "okd"], w=[("okdt", ch)])

            def state_update(St, Sk, ch, dirn):
                ps, pk = self.psum("den")
                self.mm(ps[0:64, 0:128], kd[:, ch, :], vt[:, ch, :], True, True, r=[("okdt", ch), ("ovt", ch)], w=[pk])
                col = h if dirn == 0 else 4 + h
                self.stt("dve", St[:], St[:], cdec[0:64, col:col + 1], ps[0:64, 0:128], ALU.mult, ALU.add, r=[Sk, "ocd", pk], w=[Sk])

            self.I("dve", (lambda t: lambda e: e.memset(t[:], 0.0))(Sb), w=["oSb"])
            self.I("dve", (lambda t: lambda e: e.memset(t[:], 0.0))(Sf), w=["oSf"])
            self.I("dve", (lambda t: lambda e: e.memset(t[:], 0.0))(Sfb), w=["oSfb"])
            btiles = [c.tiles[-1]] + list(reversed(c.tiles[:-1]))
            for (off, n, is_x) in btiles:
                kv_tile(off, n, is_x, 1)
                for ch in reversed(range(n // 128)):
                    gch = off // 128 + ch
                    self.act(snaps[:, gch, :], Sb[:], AF.Copy, r=["oSb"], w=[(snk, gch)])
                    state_update(Sb, "oSb", ch, 1)
            ftiles = [c.tiles[-1]] + list(c.tiles[:-1])
            for (off, n, is_x) in ftiles:
                kv_tile(off, n, is_x, 0)
                want_out = is_x or not last
                nch = n // 128
                if want_out:
                    ps, pk = self.proj_fm(wq, "owq", 0, 64, xn, xk, off, n)
                    self.rope(ps[0:64, 0:n], pk, off, n, qrT[:, 0:n], "oqrT", rtmp, 1.0, is_x)
                    for ch in range(nch):
                        self.tt("dve", qf[:, ch * 128:(ch + 1) * 128], qrT[:, ch * 128:(ch + 1) * 128], qdf[:, h, :], ALU.mult, r=["oqrT", "oqd"], w=[("oqf", ch)])
                        self.tt("pool", qb[:, ch * 128:(ch + 1) * 128], qrT[:, ch * 128:(ch + 1) * 128], qdb[:, h, :], ALU.mult, r=["oqrT", "oqd"], w=[("oqb", ch)])
                    ps, pk = self.proj_fm(wg, "owg", 0, 128, xn, xk, off, n)
                    self.act(sg[:, 0:n], ps[:, 0:n], AF.Silu, r=[pk], w=["osg"])
                    pY, pYk = self.psum("acc")
                for ch in range(nch):
                    gch = off // 128 + ch
                    if want_out:
                        ps, pk = self.psum("mm")
                        self.mm(ps[:, 0:128], krT[:, ch * 128:(ch + 1) * 128], qrT[:, ch * 128:(ch + 1) * 128], True, True, r=["okrT", "oqrT"], w=[pk])
                        A = At[ch % 2]
                        Ak = f"oAt{ch % 2}"
                        self.tt("dve", A[:], ps[:, 0:128], Dc[:, h, :], ALU.mult, r=[pk, "oDc"], w=[Ak])
                        ysl = pY[:, ch * 128:(ch + 1) * 128]
                        self.mm(ysl, vt[:, ch, :], A[:], True, False, r=[("ovt", ch), Ak], w=[pYk])
                        self.mm(ysl, Sfb[:], qf[:, ch * 128:(ch + 1) * 128], False, False, r=["oSfb", ("oqf", ch)], w=[pYk])
                        self.mm(ysl, snaps[:, gch, :], qb[:, ch * 128:(ch + 1) * 128], False, True, r=[(snk, gch), ("oqb", ch)], w=[pYk])
                    state_update(Sf, "oSf", ch, 0)
                    self.act(Sfb[:], Sf[:], AF.Copy, r=["oSf"], w=["oSfb"])
                if not want_out:
                    continue
                self.act(yf[:, 0:n], pY[:, 0:n], AF.Copy, r=[pYk], w=["oyf"])
                self.act(ysq[:, 0:n], pY[:, 0:n], AF.Square, r=[pYk], w=["oysq"])
                pd, pdk = self.psum("den")
                self.mm(pd[:, 0:n], self.ones(128, 128), ysq[:, 0:n], True, True, r=["oysq", "CSTB"], w=[pdk])
                self.rstd(pd[:, 0:n], rs[:, 0:n], 1.0 / 128, r=[pdk], w=["ors"])
                self.stt("dve", yf[:, 0:n], yf[:, 0:n], rnv[:, h:h + 1], rs[:, 0:n], ALU.mult, ALU.mult, r=["oyf", "ornv", "ors"], w=["oyf"])
                self.tt("pool", yg[:, 0:n], yf[:, 0:n], sg[:, 0:n], ALU.mult, r=["oyf", "osg"], w=["oyg"])
                for cc in range(8):
                    po, pok = self.psum("acc")
                    self.mm(po[:, 0:n], wo[:, cc * 128:(cc + 1) * 128], yg[:, 0:n], True, True, r=["owo", "oyg"], w=[pok])
                    self.resid(po[:, 0:n], pok, l, 2, b, cc, off, n, is_x)
            self.p.barrier()
            self.ar.release(m1)
            rtmp = self.rope_tmp()
        for g in range(2):
            self.ar.release(m1)
            rtmp = self.rope_tmp()
            wk = self.ar.alloc([128, 8, 64], BF16)
            wv = self.ar.alloc([128, 8, 64], BF16)
            wq = self.ar.alloc([128, 8, 256], BF16)
            wo = self.ar.alloc([64, 4, 1024], BF16)
            self.load("pool", wk[:], CI[:, :, 768 + g * 64:768 + (g + 1) * 64], "owk", "owk")
            self.load("pool", wv[:], CI[:, :, 896 + g * 64:896 + (g + 1) * 64], "owv", "owv")
            self.load("pool", wq[:], CI[:, :, 1792 + g * 256:1792 + (g + 1) * 256], "owq", "owq")
            self.load("pool", wo[:], d["cd_out"][j][512 + g * 256:512 + (g + 1) * 256, :].rearrange("(hh dd) n -> dd hh n", dd=64), "owo2", "owo2")
            sK = self.ar.alloc([64, T], BF16)
            sV = self.ar.alloc([128, NCH, 64], BF16)
            sKk, sVk = self.k("sK"), self.k("sV")
            sqf = self.ar.alloc([64, 2048], BF16)
            sq = sqf[:, :].rearrange("p (bb hh q) -> p bb hh q", bb=4, hh=4)
            os_ = self.ar.alloc([64, 4, 512], BF16)
            Pt = [self.ar.alloc([128, 512], BF16) for _ in range(3)]
            rd = self.ar.alloc([64, 512], F32)
            for (off, n, is_x) in c.tiles:
                ps, pk = self.proj_fm(wk, "owk", 0, 64, xn, xk, off, n)
                self.rope(ps[0:64, 0:n], pk, off, n, sK[:, off:off + n], (sKk, off), rtmp, 1.0, is_x)
                for ch in range(n // 128):
                    gch = off // 128 + ch
                    ps, pk = self.psum("mm")
                    for kc in range(8):
                        self.mm(ps[:, 0:64], xn[:, kc, gch * 128:(gch + 1) * 128], wv[:, kc, :], kc == 0, kc == 7, r=["owv", (xk, kc, off)], w=[pk])
                    self.act(sV[:, gch, :], ps[:, 0:64], AF.Copy, r=[pk], w=[(sVk, gch)])
            pi = 0
            for (off, n, is_x) in c.tiles:
                if (not is_x) and last:
                    continue
                nblk = n // 128
                for hh in range(4):
                    ps, pk = self.proj_fm(wq, "owq", hh * 64, 64, xn, xk, off, n)
                    self.rope(ps[0:64, 0:n], pk, off, n, sq[:, 0:nblk, hh, :], ("osq", hh), rtmp, 1.0, is_x)
                for blk in range(nblk):
                    gi = off // 128 + blk
                    if is_x:
                        kl = []
                        if gi - 1 >= 0:
                            kl.append((gi - 1, MLE))
                        kl.append((gi, None))
                        if gi + 1 < XCH:
                            kl.append((gi + 1, MGE))
                        kl += [(XCH, None), (XCH + 1, None)]
                    else:
                        kl = [(XCH, None), (XCH + 1, None)]
                    pO, pOk = self.psum("acc")
                    pD, pDk = self.psum("den")
                    qrhs = sqf[:, blk * 512:(blk + 1) * 512]
                    for i, (kc, msk) in enumerate(kl):
                        koff = (kc * 128) // 512 * 512 if kc < XCH else S
                        ps, pk = self.psum("mm")
                        self.mm(ps[:, 0:512], sK[:, kc * 128:(kc + 1) * 128], qrhs, True, True, r=[(sKk, koff)] + [("osq", hh) for hh in range(4)], w=[pk])
                        P = Pt[pi % 3]
                        Pk = f"oPt{pi % 3}"
                        pi += 1
                        self.act(P[:], ps[:, 0:512], AF.Exp, r=[pk], w=[Pk], scale=0.125)
                        if msk is not None:
                            self.tt("pool", P[:], P[:], msk, ALU.mult, r=[Pk, "CST"], w=[Pk])
                        self.mm(pO[0:64, 0:512], sV[:, kc, :], P[:], i == 0, i == len(kl) - 1, r=[(sVk, kc), Pk], w=[pOk])
                        self.mm(pD[0:64, 0:512], self.ones(128, 64), P[:], i == 0, False, r=["CSTB", Pk], w=[pDk])
                    self.mm(pD[0:64, 0:512], self.ones(1, 64), sinkr[0:1, g * 512:(g + 1) * 512], False, True, r=["CSTB", "osinkr"], w=[pDk])
                    self.recip(rd[:], pD[0:64, 0:512], r=[pDk], w=["ord"])
                    self.tt("dve", os_[:, :, blk * 128:(blk + 1) * 128], pO[0:64, 0:512].rearrange("p (hh q) -> p hh q", hh=4), rd[:].rearrange("p (hh q) -> p hh q", hh=4),
                            ALU.mult, r=[pOk, "ord"], w=[("oos", blk)])
                for cc in range(8):
                    po, pok = self.psum("acc")
                    for hh in range(4):
                        self.mm(po[:, 0:n], wo[:, hh, cc * 128:(cc + 1) * 128], os_[:, hh, 0:n], hh == 0, hh == 3, r=["owo2"] + [("oos", bb) for bb in range(nblk)], w=[pok])
                    self.resid(po[:, 0:n], pok, l, 2, b, cc, off, n, is_x)
            self.p.barrier()
        self.ar.release(m0)

    def build(self):
        c = self.cfg
        d = self.d
        S, T = c.S, c.T
        self.setup()
        allx = [("XH", cc, off) for cc in range(8) for (off, n, is_x) in c.tiles if is_x]
        allh = [("XH", cc, S) for cc in range(8)]
        for b in range(c.NB):
            self.load("sp", self.XH[:, :, 0:S], d["xT"][b].rearrange("(c p) t -> p c t", p=128), None, "xload", keys=allx)
            self.load("sp", self.XH[:, :, S:T], d["ctxT"][b].rearrange("(c p) t -> p c t", p=128), None, "hload", keys=allh)
            import os
            skip = os.environ.get("KSKIP", "").split(",")
            for l in range(c.DEPTH):
                if l % 2 == 0:
                    if "even" not in skip:
                        self.even_mixer(l, b)
                else:
                    if "odd" not in skip:
                        self.odd_mixer(l, b)
                if "ffn" not in skip:
                    self.ffn(l, b)
            m = self.ar.mark()
            sq = [self.ar.alloc([128, 512], BF16) for _ in range(2)]
            rs = self.ar.alloc([128, 512], F32)
            ob = [self.ar.alloc([128, 8, 512], F32) for _ in range(2)]
            oT = self.outT[b].rearrange("(c p) t -> p c t", p=128)
            for ti, (off, n, is_x) in enumerate(c.tiles):
                if not is_x:
                    continue
                pd, pk = self.psum("den")
                for cc in range(8):
                    s = sq[cc % 2]
                    sk = f"nsq{cc % 2}"
                    self.act(s[:, 0:n], self.XH[:, cc, off:off + n], AF.Square, r=[("XH", cc, off)], w=[sk])
                    self.mm(pd[:, 0:n], self.ones(128, 128), s[:, 0:n], cc == 0, cc == 7, r=[sk, "CSTB"], w=[pk])
                self.rstd(pd[:, 0:n], rs[:, 0:n], 1.0 / D, r=[pk], w=["nrs"])
                o = ob[ti % 2]
                ok = f"ob{ti % 2}"
                for cc in range(8):
                    self.stt("dve", o[:, cc, 0:n], self.XH[:, cc, off:off + n], self.NFIN[:, cc:cc + 1], rs[:, 0:n], ALU.mult, ALU.mult,
                             r=[("XH", cc, off), "NFIN", "nrs"], w=[(ok, cc)])
                for cc in range(8):
                    self.I("sp", (lambda o, off, n, cc, oT: lambda e: e.dma_start(out=oT[:, cc, off:off + n], in_=o[:, cc, 0:n]))(o, off, n, cc, oT),
                           r=[(ok, cc)], dsem=ok, cont=(cc > 0))
            self.p.barrier()
            self.ar.release(m)
        self.p.barrier()
        self.p.emit()
        return self.nc


def _consts(S):
    cst = np.zeros((128, 2048), np.float32)
    cst[:, 0:128] = 1.0
    cst[:, 128:256] = np.eye(128, dtype=np.float32)
    for i in range(32):
        cst[32 + i, 256 + i] = -1.0
        cst[i, 256 + 32 + i] = 1.0
    k = np.arange(128)[:, None].astype(np.float32)
    q = np.arange(128)[None, :].astype(np.float32)
    cst[:, 320:448] = np.maximum(q - k, 0)
    cst[:, 448:576] = np.maximum(k - q, 0)
    cst[:, 576:1088] = np.tile((q >= k).astype(np.float32), (1, 4))
    cst[:, 1088:1600] = np.tile((k >= q).astype(np.float32), (1, 4))
    cst[:, 1600:1728] = np.arange(128, dtype=np.float32)[None, :] + 1.0
    cst[:, 1728:1856] = 128.0 - np.arange(128, dtype=np.float32)[None, :]
    cst[:, 1856] = 127.0 - np.arange(128, dtype=np.float32)
    cst[:, 1857] = np.arange(128, dtype=np.float32)
    t = np.arange(S)
    row = (t // 64).astype(np.float32)
    col = (t % 64).astype(np.float32)
    freqs = (np.float32(10000.0) ** (-np.arange(16, dtype=np.float32) / np.float32(16))).astype(np.float32)
    ang = np.concatenate([row[:, None] * freqs, col[:, None] * freqs], axis=-1).astype(np.float32)
    C = np.cos(ang).astype(np.float32).T
    Sn = np.sin(ang).astype(np.float32).T
    return cst, np.ascontiguousarray(np.concatenate([C, C], 0)), np.ascontiguousarray(np.concatenate([Sn, Sn], 0))


def _fm(v):
    v = np.asarray(v, np.float32)
    v2 = v.reshape(-1, v.shape[-1] // 128, 128)
    return np.ascontiguousarray(v2.transpose(2, 0, 1).reshape(128, -1))


_CACHE = {}


def run(inputs, S, NBTOT, DEPTH, n_cores):
    NB = NBTOT // n_cores
    cfg = Cfg(S=S, NB=NB, DEPTH=DEPTH)
    key = (S, NB, DEPTH)
    if key not in _CACHE:
        _CACHE[key] = Builder(cfg).build()
    nc = _CACHE[key]
    f = lambda a: np.ascontiguousarray(np.asarray(a, np.float32))
    NE, NO = cfg.NE, max(cfg.NO, 1)
    cst, rC, rS = _consts(S)
    x, c, ctx, c_ctx = inputs["x"], inputs["c"], inputs["ctx"], inputs["c_ctx"]
    shared = dict(
        ada_w=f(inputs["ada_w"]), ada_bT=_fm(inputs["ada_b"]), nmixT=_fm(inputs["norm_mix"]), nffnT=_fm(inputs["norm_ffn"]),
        nfinT=_fm(inputs["norm_final"]), ffn_in=f(inputs["ffn_in"]), ffn_out=f(inputs["ffn_out"]),
        ab_in=f(inputs["ab_in"]), ab_out=f(inputs["ab_out"]), qnT=_fm(inputs["mla_q_norm"]), kvnT=_fm(inputs["mla_kv_norm"]),
        wq_b=f(inputs["mla_wq_b"]), wkv_b=f(inputs["mla_wkv_b"]),
        vnorm_bc=np.ascontiguousarray(np.broadcast_to(f(inputs["cmlp_v_norm"]).reshape(1, -1), (128, NE * 512))),
        wsT=np.ascontiguousarray(f(inputs["cmlp_ws"]).transpose(0, 3, 1, 2).reshape(NE, 128, 512)),
        bs_row=f(inputs["cmlp_bs"]).reshape(1, -1),
        cd_in=f(inputs["cd_in"]), cd_out=f(inputs["cd_out"]),
        dec_bc=np.ascontiguousarray(np.broadcast_to(np.concatenate([f(inputs["ret_decay_fwd"]), f(inputs["ret_decay_bwd"])], axis=1).reshape(1, -1), (128, NO * 8))),
        retnT=_fm(inputs["ret_norm"]),
        sink_row=np.ascontiguousarray(np.repeat(f(inputs["swa_sink"]).reshape(-1), 128).reshape(1, -1)),
        ropeC=rC, ropeS=rS, cst=cst,
    )
    in_maps = []
    for i in range(n_cores):
        sl = slice(i * NB, (i + 1) * NB)
        m = dict(shared)
        m["xT"] = np.ascontiguousarray(f(x[sl]).transpose(0, 2, 1))
        m["ctxT"] = np.ascontiguousarray(f(ctx[sl]).transpose(0, 2, 1))
        m["cT"] = np.ascontiguousarray(np.concatenate([f(c[sl]), f(c_ctx)[None, :]], axis=0).T)
        in_maps.append(m)
    res = run_bass_kernel_spmd(nc, in_maps, core_ids=list(range(n_cores)))
    outs = [np.asarray(r["outT"]).transpose(0, 2, 1) for r in res.results]
    return np.ascontiguousarray(np.concatenate(outs, axis=0)).astype(np.float32)


def kernel(**inputs):
    return run(inputs, 2048, 32, 4, 8)
```

```python
import numpy as np
import concourse.bass as bass
import concourse.mybir as mybir
from concourse.bass_utils import run_bass_kernel_spmd

F32 = mybir.dt.float32
BF16 = mybir.dt.bfloat16
AF = mybir.ActivationFunctionType
ALU = mybir.AluOpType

ENGS = ("pe", "act", "dve", "pool", "sp")
SEM_CAP = 30000
import os as _os
STRICT = bool(_os.environ.get("KSTRICT"))


class Ins:
    __slots__ = ("eng", "fn", "deps", "signal", "dsem", "ticket", "idx", "semh")

    def __init__(self, eng, fn, dsem):
        self.eng = eng
        self.fn = fn
        self.deps = []
        self.signal = False
        self.dsem = dsem
        self.ticket = None
        self.semh = None
        self.idx = -1


class Prog:
    def __init__(self, nc):
        self.nc = nc
        self.q = {e: [] for e in ENGS}
        self.res = {}
        self.dma_last = {}

    def ins(self, eng, fn, r=(), w=(), dsem=None, cont=False):
        I = Ins(eng, fn, dsem)
        I.idx = len(self.q[eng])
        self.q[eng].append(I)
        if dsem is not None:
            prev = self.dma_last.get(dsem)
            if prev is not None and not cont:
                self._dep(I, prev, True)
            self.dma_last[dsem] = I
        for k in r:
            st = self.res.get(k)
            assert st is not None and st[0] is not None, f"read before write: {k}"
            self._dep(I, st[0], True)
            if isinstance(k, str) and k.startswith("ps"):
                for rd in st[1]:
                    if rd.eng != I.eng:
                        self._dep(I, rd, True)
            st[1].append(I)
        for k in w:
            st = self.res.get(k)
            if st is None:
                st = [None, []]
                self.res[k] = st
            if st[0] is not None and not cont:
                self._dep(I, st[0], False)
            lastc = {}
            for rd in st[1]:
                if rd.dsem is not None:
                    self._dep(I, rd, False)
                elif rd.eng not in lastc or lastc[rd.eng].idx < rd.idx:
                    lastc[rd.eng] = rd
            for rd in lastc.values():
                self._dep(I, rd, False)
            st[0] = I
            st[1] = []
        return I

    def _dep(self, I, D, raw):
        if D is I:
            return
        if D.eng == I.eng and D.dsem is None and I.dsem is None and not STRICT:
            if not raw or I.idx - D.idx > 4:
                return
        D.signal = True
        I.deps.append(D)

    def barrier(self):
        lasts = []
        for e in ENGS:
            if self.q[e]:
                for I in reversed(self.q[e]):
                    if I.dsem is None and I.fn is not None:
                        lasts.append(I)
                        break
        lasts += [I for I in self.dma_last.values()]
        for e in ENGS:
            W = Ins(e, None, None)
            W.idx = len(self.q[e])
            self.q[e].append(W)
            for D in lasts:
                if D.eng == e and D.dsem is None:
                    continue
                D.signal = True
                W.deps.append(D)

    def emit(self):
        nc = self.nc
        esems = {}
        for e in ENGS:
            cnt = 0
            ep = 0
            for I in self.q[e]:
                if I.dsem is None and I.signal:
                    if cnt >= SEM_CAP:
                        ep += 1
                        cnt = 0
                    cnt += 1
                    key = (e, ep)
                    if key not in esems:
                        esems[key] = nc.alloc_semaphore(f"s_{e}_{ep}")
                    I.semh = esems[key]
                    I.ticket = cnt
        dsems = {}
        dcnt = {}
        for e in ENGS:
            for I in self.q[e]:
                if I.dsem is not None:
                    if I.dsem not in dsems:
                        dsems[I.dsem] = nc.alloc_semaphore(f"d_{I.dsem}")
                        dcnt[I.dsem] = 0
                    dcnt[I.dsem] += 16
                    I.semh = dsems[I.dsem]
                    I.ticket = dcnt[I.dsem]
        q = self.q

        def replay(ename, eng):
            waited = {}
            for I in q[ename]:
                need = {}
                for D in I.deps:
                    k = D.semh.num
                    if waited.get(k, 0) >= D.ticket:
                        continue
                    if need.get(k, (None, 0))[1] < D.ticket:
                        need[k] = (D.semh, D.ticket)
                for k, (h, t) in need.items():
                    eng.wait_ge(h, t)
                    waited[k] = t
                if I.fn is None:
                    continue
                bi = I.fn(eng)
                if I.dsem is not None:
                    bi.then_inc(I.semh, 16)
                elif I.signal:
                    bi.then_inc(I.semh, 1)

        with nc.Block() as block:
            @block.tensor
            def _(e):
                replay("pe", e)

            @block.scalar
            def _(e):
                replay("act", e)

            @block.vector
            def _(e):
                replay("dve", e)

            @block.gpsimd
            def _(e):
                replay("pool", e)

            @block.sync
            def _(e):
                replay("sp", e)


D = 1024
CT = 256
FH = 2816
EPS = 1e-6
L_EVEN_IN = 1600
L_ODD_IN = 2304


class Cfg:
    def __init__(self, S=2048, NB=4, DEPTH=4):
        self.S = S
        self.NB = NB
        self.DEPTH = DEPTH
        self.T = S + CT
        self.NE = (DEPTH + 1) // 2
        self.NO = DEPTH // 2
        self.tiles = [(j * 512, 512, True) for j in range(S // 512)] + [(S, CT, False)]


class Arena:
    def __init__(self, nc, base, top):
        self.nc = nc
        self.base = (base + 63) // 64 * 64
        self.top = top
        self.cur = self.base
        self.n = 0

    def alloc(self, shape, dt):
        nb = int(np.prod(shape[1:])) * (4 if dt == F32 else 2)
        nb = (nb + 63) // 64 * 64
        off = self.cur
        assert off + nb <= self.top, f"arena overflow {off + nb - self.top}"
        self.cur += nb
        self.n += 1
        return self.nc.alloc_sbuf_tensor_at(f"ar{self.n}", list(shape), dt, offset=off)

    def mark(self):
        return self.cur

    def release(self, m):
        self.cur = m


class Builder:
    def __init__(self, cfg):
        self.cfg = cfg
        nc = bass.Bass("TRN2", target_bir_lowering=False)
        self.nc = nc
        self.p = Prog(nc)
        self.uid = 0
        c = cfg
        NB, S, T, L = c.NB, c.S, c.T, c.DEPTH
        di = lambda n, sh: nc.dram_tensor(n, list(sh), F32, kind="ExternalInput").ap()
        self.d = dict(
            xT=di("xT", [NB, D, S]), ctxT=di("ctxT", [NB, D, CT]), cT=di("cT", [D, NB + 1]),
            ada_w=di("ada_w", [L, D, 6 * D]), ada_bT=di("ada_bT", [128, L * 48]),
            nmixT=di("nmixT", [128, L * 8]), nffnT=di("nffnT", [128, L * 8]), nfinT=di("nfinT", [128, 8]),
            ffn_in=di("ffn_in", [L, D, 2 * FH]), ffn_out=di("ffn_out", [L, FH, D]),
            ab_in=di("ab_in", [c.NE, D, 1600]), ab_out=di("ab_out", [c.NE, D, D]),
            qnT=di("qnT", [128, c.NE * 2]), kvnT=di("kvnT", [128, c.NE * 2]),
            wq_b=di("wq_b", [c.NE, 256, 768]), wkv_b=di("wkv_b", [c.NE, 256, 1024]),
            vnorm_bc=di("vnorm_bc", [128, c.NE * 512]), wsT=di("wsT", [c.NE, 128, 4 * 128]),
            bs_row=di("bs_row", [1, c.NE * 512]),
            cd_in=di("cd_in", [max(c.NO, 1), D, 2304]), cd_out=di("cd_out", [max(c.NO, 1), D, D]),
            dec_bc=di("dec_bc", [128, max(c.NO, 1) * 8]), retnT=di("retnT", [128, max(c.NO, 1) * 4]),
            sink_row=di("sink_row", [1, max(c.NO, 1) * 1024]),
            ropeC=di("ropeC", [64, S]), ropeS=di("ropeS", [64, S]),
            cst=di("cst", [128, 2048]),
        )
        self.outT = nc.dram_tensor("outT", [NB, D, S], F32, kind="ExternalOutput").ap()
        al = nc.alloc_sbuf_tensor
        self.XH = al("XH", [128, 8, T], F32)
        self.MODS = al("MODS", [128, L, 6, 8, NB + 1], F32)
        self.AM = al("AM", [128, L, 2, 8, NB + 1], F32)
        self.NMIX = al("NMIX", [128, L * 8], F32)
        self.NFFN = al("NFFN", [128, L * 8], F32)
        self.NFIN = al("NFIN", [128, 8], F32)
        self.ADAB = al("ADAB", [128, L * 48], F32)
        self.CST = al("CST", [128, 2048], F32)
        self.CSTB = al("CSTB", [128, 512], BF16)
        self.ROPC = al("ROPC", [64, S], BF16)
        self.ROPS = al("ROPS", [64, S], BF16)
        self.EPSC = al("EPSC", [128, 1], F32)
        self.ps = {}
        self.psi = {}
        for pool, n in (("mm", 3), ("acc", 2), ("den", 2)):
            self.ps[pool] = [nc.alloc_psum_tensor(f"ps_{pool}{i}", [128, 512], F32) for i in range(n)]
            self.psi[pool] = 0
        self.pst = nc.alloc_psum_tensor("ps_tr", [128, 1024], BF16)
        self.ar = Arena(nc, nc.sbuf_base + 64, nc.sbuf_top)

    def k(self, s):
        self.uid += 1
        return f"{s}#{self.uid}"

    def psum(self, pool):
        i = self.psi[pool]
        self.psi[pool] = (i + 1) % len(self.ps[pool])
        return self.ps[pool][i], f"ps_{pool}{i}"

    def I(self, eng, fn, r=(), w=(), dsem=None, cont=False):
        return self.p.ins(eng, fn, r, w, dsem, cont)

    def load(self, q, dst_ap, src_ap, key, dsem, keys=None):
        w = keys if keys is not None else [key]
        if len(dst_ap.shape) == 3 and dst_ap.shape[1] > 1:
            for i in range(dst_ap.shape[1]):
                self.I(q, (lambda i: lambda e: e.dma_start(out=dst_ap[:, i, :], in_=src_ap[:, i, :]))(i), w=w, dsem=dsem, cont=(i > 0))
        else:
            self.I(q, lambda e: e.dma_start(out=dst_ap, in_=src_ap), w=w, dsem=dsem)

    def mm(self, out, lhsT, rhs, start, stop, r, w):
        self.I("pe", lambda e: e.matmul(out, lhsT, rhs, start=start, stop=stop), r=r, w=w)

    def act(self, out, in_, func, r, w, scale=1.0, bias=None, accum=None):
        kw = dict(out=out, in_=in_, func=func, scale=scale)
        if bias is not None:
            kw["bias"] = bias
        if accum is not None:
            kw["accum_out"] = accum
        self.I("act", lambda e: e.activation(**kw), r=r, w=w)

    def tt(self, eng, out, in0, in1, op, r, w):
        self.I(eng, lambda e: e.tensor_tensor(out=out, in0=in0, in1=in1, op=op), r=r, w=w)

    def stt(self, eng, out, in0, scalar, in1, op0, op1, r, w):
        self.I(eng, lambda e: e.scalar_tensor_tensor(out=out, in0=in0, scalar=scalar, in1=in1, op0=op0, op1=op1), r=r, w=w)

    def recip(self, out, in_, r, w):
        self.I("dve", lambda e: e.reciprocal(out=out, in_=in_), r=r, w=w)

    def rstd(self, ps_ap, out_ap, inv_n, r, w):
        self.act(out_ap, ps_ap, AF.Sqrt, r=r + ["EPSC"], w=w, scale=inv_n, bias=self.EPSC[0:out_ap.shape[0], 0:1])
        self.recip(out_ap, out_ap, r=w, w=w)

    def ones(self, kp, m):
        return self.CSTB[0:kp, 0:m]

    def ident(self, n):
        return self.CSTB[0:n, 128:128 + n]

    def RT(self):
        return self.CSTB[0:64, 256:320]

    def setup(self):
        c = self.cfg
        d = self.d
        self.load("sp", self.CST[:], d["cst"], "CST", "cst")
        self.I("pool", lambda e: e.dma_start(out=self.CSTB[:, 0:320], in_=d["cst"][:, 0:320]), w=["CSTB"], dsem="cstb")
        self.I("pool", lambda e: e.dma_start(out=self.ROPC[:], in_=d["ropeC"]), w=["ROPC"], dsem="ropc")
        self.I("pool", lambda e: e.dma_start(out=self.ROPS[:], in_=d["ropeS"]), w=["ROPS"], dsem="rops")
        self.load("sp", self.NMIX[:], d["nmixT"], "NMIX", "nmix")
        self.load("sp", self.NFFN[:], d["nffnT"], "NFFN", "nffn")
        self.load("sp", self.NFIN[:], d["nfinT"], "NFIN", "nfin")
        self.load("sp", self.ADAB[:], d["ada_bT"], "ADAB", "adab")
        self.I("dve", lambda e: e.memset(self.EPSC[:], EPS), w=["EPSC"])
        NB1 = c.NB + 1
        m = self.ar.mark()
        cT = self.ar.alloc([128, 8, NB1], F32)
        cB = self.ar.alloc([128, 8, NB1], BF16)
        self.load("sp", cT[:], d["cT"].rearrange("(c p) n -> p c n", p=128), "cT", "ct")
        self.act(cB[:], cT[:], AF.Silu, r=["cT"], w=["cB"])
        slabs = [self.ar.alloc([128, 8, 1024], BF16) for _ in range(2)]
        n = 0
        for l in range(c.DEPTH):
            for wv in range(6):
                sl = slabs[n % 2]
                sk = f"adaslab{n % 2}"
                src = d["ada_w"][l].rearrange("(kc p) n -> p kc n", p=128)[:, :, wv * 1024:(wv + 1) * 1024]
                self.load("pool", sl[:], src, sk, sk)
                n += 1
                for oc in range(8):
                    ps, pk = self.psum("mm")
                    for kc in range(8):
                        self.mm(ps[:, 0:NB1], sl[:, kc, oc * 128:(oc + 1) * 128], cB[:, kc, :], kc == 0, kc == 7, r=[sk, "cB"], w=[pk])
                    col = l * 48 + wv * 8 + oc
                    self.act(self.MODS[:, l, wv, oc, :], ps[:, 0:NB1], AF.Identity, r=[pk, "ADAB"], w=["MODS"], bias=self.ADAB[:, col:col + 1])
        for l in range(c.DEPTH):
            for which, (nt, nk, wsc) in enumerate(((self.NMIX, "NMIX", 1), (self.NFFN, "NFFN", 4))):
                for oc in range(8):
                    col = l * 8 + oc
                    self.I("dve", (lambda l, which, oc, nt, col, wsc: lambda e: e.tensor_scalar(
                        out=self.AM[:, l, which, oc, :], in0=self.MODS[:, l, wsc, oc, :], scalar1=1.0, scalar2=nt[:, col:col + 1],
                        op0=ALU.add, op1=ALU.mult))(l, which, oc, nt, col, wsc), r=["MODS", nk], w=["AM"])
        self.p.barrier()
        self.ar.release(m)

    def norm_tile(self, l, which, b, off, n, is_x, dst, dst_off, dkey, tmp):
        col = b if is_x else self.cfg.NB
        sq, rs, tf = tmp["sq"], tmp["rs"], tmp["tf"]
        pd, pk = self.psum("den")
        for cc in range(8):
            s = sq[cc % 2]
            sk = f"nsq{cc % 2}"
            self.act(s[:, 0:n], self.XH[:, cc, off:off + n], AF.Square, r=[("XH", cc, off)], w=[sk])
            self.mm(pd[:, 0:n], self.ones(128, 128), s[:, 0:n], cc == 0, cc == 7, r=[sk, "CSTB"], w=[pk])
        self.rstd(pd[:, 0:n], rs[:, 0:n], 1.0 / D, r=[pk], w=["nrs"])
        wsh = 0 if which == 0 else 3
        for cc in range(8):
            t = tf[cc % 2]
            tk = f"ntf{cc % 2}"
            self.stt("dve", t[:, 0:n], self.XH[:, cc, off:off + n], self.AM[:, l, which, cc, col:col + 1], rs[:, 0:n],
                     ALU.mult, ALU.mult, r=[("XH", cc, off), "AM", "nrs"], w=[tk])
            self.act(dst[:, cc, dst_off:dst_off + n], t[:, 0:n], AF.Identity, r=[tk, "MODS"], w=[(dkey, cc, dst_off)],
                     bias=self.MODS[:, l, wsh, cc, col:col + 1])

    def norm_tmp(self):
        return dict(sq=[self.ar.alloc([128, 512], BF16) for _ in range(2)], rs=self.ar.alloc([128, 512], F32),
                    tf=[self.ar.alloc([128, 512], F32) for _ in range(2)])

    def resid(self, ps, pk, l, wg, b, cc, off, n, is_x, extra_r=()):
        col = b if is_x else self.cfg.NB
        self.stt("dve", self.XH[:, cc, off:off + n], ps, self.MODS[:, l, wg, cc, col:col + 1], self.XH[:, cc, off:off + n],
                 ALU.mult, ALU.add, r=[pk, "MODS", ("XH", cc, off)] + list(extra_r), w=[("XH", cc, off)])

    def rope(self, ps_ap, pk, off, n, dst_ap, dkey, tmp, scale=1.0, is_x=True):
        v3 = (lambda a: a.rearrange("p (a q) -> p a q", q=128)) if len(dst_ap.shape) == 3 else (lambda a: a)
        if not is_x or _os.environ.get("KROPECOPY"):
            self.act(dst_ap, v3(ps_ap), AF.Copy, r=[pk], w=[dkey], scale=scale)
            return
        i = tmp["i"] = (tmp.get("i", 0) + 1) % len(tmp["zb"])
        zb, t1, t2 = tmp["zb"][i], tmp["t1"][i], tmp["t2"][i]
        tg = tmp.get("tag", "")
        lvl = int(_os.environ.get("KROPELVL", "9")) if tmp.get("dbg") else 9
        self.act(zb[:, 0:n], ps_ap, AF.Copy, r=[pk], w=[f"rzb{tg}{i}"], scale=scale)
        if lvl >= 2:
            var = _os.environ.get("KSTTVAR", "") if tmp.get("dbg") else ""
            in0 = zb[:, 0:n] if var == "in0" else ps_ap
            in1 = zb[:, 0:n] if var == "in1" else self.ROPC[:, off:off + n]
            self.stt("dve", t1[:, 0:n], in0, scale, in1, ALU.mult, ALU.mult, r=[pk, "ROPC", f"rzb{tg}{i}"], w=[f"rt1{tg}{i}"])
        if lvl >= 3:
            pr, prk = self.psum("den")
            self.mm(pr[0:64, 0:n], self.RT(), zb[:, 0:n], True, True, r=[f"rzb{tg}{i}", "CSTB"], w=[prk])
        if lvl >= 4:
            self.tt("dve", t2[:, 0:n], pr[0:64, 0:n], self.ROPS[:, off:off + n], ALU.mult, r=[prk, "ROPS"], w=[f"rt2{tg}{i}"])
        if lvl < 5:
            self.act(dst_ap, v3(ps_ap), AF.Copy, r=[pk], w=[dkey], scale=scale)
            return
        self.tt("dve" if _os.environ.get("KDVEADD") else "pool", dst_ap, v3(t1[:, 0:n]), v3(t2[:, 0:n]), ALU.add, r=[f"rt1{tg}{i}", f"rt2{tg}{i}"], w=[dkey])

    def rope_tmp(self, nbuf=2, tag=""):
        return dict(zb=[self.ar.alloc([64, 512], BF16) for _ in range(nbuf)], t1=[self.ar.alloc([64, 512], F32) for _ in range(nbuf)],
                    t2=[self.ar.alloc([64, 512], F32) for _ in range(nbuf)], tag=tag)

    def ffn(self, l, b):
        c = self.cfg
        d = self.d
        T = c.T
        self.p.barrier()
        m = self.ar.mark()
        ntmp = self.norm_tmp()
        xn = self.ar.alloc([128, 8, T], BF16)
        xk = self.k("fxn")
        ftiles = [t for t in c.tiles if t[2] or l < c.DEPTH - 1]
        for (off, n, is_x) in ftiles:
            self.norm_tile(l, 1, b, off, n, is_x, xn, off, xk, ntmp)
        g = [self.ar.alloc([128, 2, T], BF16) for _ in range(2)]
        win = [self.ar.alloc([128, 8, 512], BF16) for _ in range(2)]
        wout = [self.ar.alloc([128, 2, 1024], BF16) for _ in range(2)]
        sa = [self.ar.alloc([128, 512], BF16) for _ in range(2)]
        fin = d["ffn_in"][l].rearrange("(kc p) n -> p kc n", p=128)
        NG = FH // 256
        for grp in range(NG):
            s = grp % 2
            j0 = grp * 256
            self.load("pool", win[s][:, :, 0:256], fin[:, :, j0:j0 + 256], f"fwa{s}", f"fwa{s}")
            self.load("pool", win[s][:, :, 256:512], fin[:, :, FH + j0:FH + j0 + 256], f"fwb{s}", f"fwb{s}")
            self.load("pool", wout[s][:], d["ffn_out"][l][j0:j0 + 256, :].rearrange("(blk p) n -> p blk n", p=128), f"fwo{s}", f"fwo{s}")
            gk = f"fg{s}"
            for (off, n, is_x) in ftiles:
                for blk in range(2):
                    pa, pak = self.psum("mm")
                    pb, pbk = self.psum("mm")
                    for kc in range(8):
                        self.mm(pa[:, 0:n], win[s][:, kc, blk * 128:(blk + 1) * 128], xn[:, kc, off:off + n], kc == 0, kc == 7, r=[f"fwa{s}", (xk, kc, off)], w=[pak])
                    for kc in range(8):
                        self.mm(pb[:, 0:n], win[s][:, kc, 256 + blk * 128:256 + (blk + 1) * 128], xn[:, kc, off:off + n], kc == 0, kc == 7, r=[f"fwb{s}", (xk, kc, off)], w=[pbk])
                    si = (blk) % 2
                    self.act(sa[si][:, 0:n], pa[:, 0:n], AF.Silu, r=[pak], w=[f"fsa{si}"])
                    self.tt("dve", g[s][:, blk, off:off + n], pb[:, 0:n], sa[si][:, 0:n], ALU.mult, r=[pbk, f"fsa{si}"], w=[(gk, blk, off)])
            for (off, n, is_x) in ftiles:
                for cc in range(8):
                    po, pok = self.psum("acc")
                    for blk in range(2):
                        self.mm(po[:, 0:n], wout[s][:, blk, cc * 128:(cc + 1) * 128], g[s][:, blk, off:off + n], blk == 0, blk == 1, r=[f"fwo{s}", (gk, blk, off)], w=[pok])
                    self.resid(po[:, 0:n], pok, l, 5, b, cc, off, n, is_x)
        self.p.barrier()
        self.ar.release(m)

    def proj_fm(self, W, wkey, c0, M, X, xkey, xoff, n, nk=8, pool="mm"):
        ps, pk = self.psum(pool)
        for kc in range(nk):
            self.mm(ps[0:M, 0:n], W[:, kc, c0:c0 + M], X[:, kc, xoff:xoff + n], kc == 0, kc == nk - 1, r=[wkey, (xkey, kc, xoff)], w=[pk])
        return ps, pk

    def even_mixer(self, l, b):
        c = self.cfg
        d = self.d
        j = l // 2
        T, S = c.T, c.S
        last = l == c.DEPTH - 1
        self.p.barrier()
        m0 = self.ar.mark()
        kvn = self.ar.alloc([128, 2, T], BF16)
        qn = self.ar.alloc([128, 2, T], BF16)
        kpe = self.ar.alloc([64, T], BF16)
        kvk, qnk, kpk = self.k("kvn"), self.k("qn"), self.k("kpe")
        m1 = self.ar.mark()
        ntmp = self.norm_tmp()
        rtmp = self.rope_tmp()
        Win = self.ar.alloc([128, 8, 1600], BF16)
        Wo2 = self.ar.alloc([128, 4, 1024], BF16)
        self.load("pool", Win[:], d["ab_in"][j].rearrange("(kc p) n -> p kc n", p=128), "eWin", "eWin")
        self.load("pool", Wo2[:], d["ab_out"][j][512:1024, :].rearrange("(kc p) n -> p kc n", p=128), "eWo2", "eWo2")
        vnb = self.ar.alloc([128, 512], F32)
        self.load("sp", vnb[:], d["vnorm_bc"][:, j * 512:(j + 1) * 512], "evnb", "evnb")
        wsT = self.ar.alloc([128, 512], BF16)
        self.load("pool", wsT[:], d["wsT"][j], "ewsT", "ewsT")
        bsr = self.ar.alloc([1, 512], BF16)
        self.load("pool", bsr[:], d["bs_row"][:, j * 512:(j + 1) * 512], "ebsr", "ebsr")
        lnv = self.ar.alloc([128, 4], F32)
        self.load("sp", lnv[:, 0:2], d["kvnT"][:, j * 2:j * 2 + 2], "elnv0", "elnv0")
        self.load("sp", lnv[:, 2:4], d["qnT"][:, j * 2:j * 2 + 2], "elnv1", "elnv1")
        xn = self.ar.alloc([128, 8, 512], BF16)
        zf = self.ar.alloc([128, 2, 512], F32)
        sqb = self.ar.alloc([128, 2, 512], BF16)
        rs = self.ar.alloc([128, 512], F32)
        u = self.ar.alloc([128, 4, 512], BF16)
        vg = self.ar.alloc([128, 512], F32)
        sqv = self.ar.alloc([128, 512], F32)
        ss = self.ar.alloc([128, 4], F32)
        vtok = self.ar.alloc([128, 4, 512], BF16)
        cm = self.ar.alloc([128, 4, 512], BF16)
        xk = self.k("exn")
        tiles = [t for t in c.tiles]
        for (off, n, is_x) in tiles:
            self.norm_tile(l, 0, b, off, n, is_x, xn, 0, xk, ntmp)

            def latent(c0, nv0, dst, dkey):
                for blk in range(2):
                    ps, pk = self.proj_fm(Win, "eWin", c0 + blk * 128, 128, xn, xk, 0, n)
                    self.act(zf[:, blk, 0:n], ps[:, 0:n], AF.Copy, r=[pk], w=[("ezf", blk)])
                    self.act(sqb[:, blk, 0:n], ps[:, 0:n], AF.Square, r=[pk], w=[("esq", blk)])
                pd, pdk = self.psum("den")
                for blk in range(2):
                    self.mm(pd[:, 0:n], self.ones(128, 128), sqb[:, blk, 0:n], blk == 0, blk == 1, r=[("esq", blk), "CSTB"], w=[pdk])
                self.rstd(pd[:, 0:n], rs[:, 0:n], 1.0 / 256, r=[pdk], w=["ers"])
                for blk in range(2):
                    self.stt("dve", dst[:, blk, off:off + n], zf[:, blk, 0:n], lnv[:, nv0 + blk:nv0 + blk + 1], rs[:, 0:n], ALU.mult, ALU.mult,
                             r=[("ezf", blk), "elnv0", "elnv1", "ers"], w=[(dkey, blk, off)])

            latent(0, 0, kvn, kvk)
            latent(320, 2, qn, qnk)
            ps, pk = self.proj_fm(Win, "eWin", 256, 64, xn, xk, 0, n)
            self.rope(ps[0:64, 0:n], pk, off, n, kpe[:, off:off + n], (kpk, off), rtmp, 1.0, is_x)
            for g in range(4):
                ps, pk = self.proj_fm(Win, "eWin", 576 + g * 128, 128, xn, xk, 0, n)
                self.act(u[:, g, 0:n], ps[:, 0:n], AF.Gelu, r=[pk], w=[("eu", g)])
            nch = n // 128
            for ch in range(nch):
                ps, pk = self.psum("mm")
                for kc in range(8):
                    self.mm(ps[:, 0:512], xn[:, kc, ch * 128:(ch + 1) * 128], Win[:, kc, 1088:1600], kc == 0, kc == 7, r=["eWin", (xk, kc, 0)], w=[pk])
                self.act(vg[:], ps[:, 0:512], AF.Gelu, r=[pk], w=["evg"])
                self.act(sqv[:], vg[:], AF.Square, r=["evg"], w=["esqv"])
                self.I("dve", lambda e: e.tensor_reduce(out=ss[:], in_=sqv[:].rearrange("p (g dd) -> p g dd", g=4), axis=mybir.AxisListType.X, op=ALU.add), r=["esqv"], w=["ess"])
                self.rstd(ss[:], ss[:], 1.0 / 128, r=["ess"], w=["ess"])
                for g in range(4):
                    self.stt("dve", vtok[:, ch, g * 128:(g + 1) * 128], vg[:, g * 128:(g + 1) * 128], ss[:, g:g + 1], vnb[:, g * 128:(g + 1) * 128],
                             ALU.mult, ALU.mult, r=["evg", "ess", "evnb"], w=[("evt", ch)])
            for g in range(4):
                ps, pk = self.psum("mm")
                for ch in range(nch):
                    self.mm(ps[:, ch * 128:(ch + 1) * 128], vtok[:, ch, g * 128:(g + 1) * 128], wsT[:, g * 128:(g + 1) * 128], True, False, r=[("evt", ch), "ewsT"], w=[pk])
                    self.mm(ps[:, ch * 128:(ch + 1) * 128], self.ones(1, 128), bsr[0:1, g * 128:(g + 1) * 128], False, True, r=["CSTB", "ebsr"], w=[pk])
                self.tt("dve", cm[:, g, 0:n], ps[:, 0:n], u[:, g, 0:n], ALU.mult, r=[pk, ("eu", g)], w=[("ecm", g)])
            for cc in range(8):
                po, pok = self.psum("acc")
                for g in range(4):
                    self.mm(po[:, 0:n], Wo2[:, g, cc * 128:(cc + 1) * 128], cm[:, g, 0:n], g == 0, g == 3, r=["eWo2", ("ecm", g)], w=[pok])
                self.resid(po[:, 0:n], pok, l, 2, b, cc, off, n, is_x)
        self.p.barrier()
        self.ar.release(m1)
        import os
        if os.environ.get("KSTOP") == "A":
            self.ar.release(m0)
            return
        rtmpB = self.rope_tmp()
        wkv = self.ar.alloc([128, 2, 1024], BF16)
        wv = self.ar.alloc([128, 2, 4, 128], BF16)
        self.load("pool", wkv[:], d["wkv_b"][j].rearrange("(kc p) n -> p kc n", p=128), "ewkv", "ewkv")
        for kc in range(2):
            self.load("pool", wv[:, kc, :, :], d["wkv_b"][j][kc * 128:(kc + 1) * 128, :].rearrange("p (h two dd) -> p h two dd", two=2, dd=128)[:, :, 1, :], f"ewv{kc}", f"ewv{kc}")
        Kn = self.ar.alloc([128, 4, T], BF16)
        Vt = self.ar.alloc([128, T // 128, 512], BF16)
        Knk, Vtk = self.k("Kn"), self.k("Vt")
        for (off, n, is_x) in c.tiles:
            for h in range(4):
                ps, pk = self.proj_fm(wkv, "ewkv", h * 256, 128, kvn, kvk, off, n, nk=2)
                self.act(Kn[:, h, off:off + n], ps[:, 0:n], AF.Copy, r=[pk], w=[(Knk, h, off)])
            for ch in range(n // 128):
                gch = off // 128 + ch
                ps, pk = self.psum("mm")
                for kc in range(2):
                    self.mm(ps[:, 0:512], kvn[:, kc, gch * 128:(gch + 1) * 128], wv[:, kc, :, :], kc == 0, kc == 1, r=[f"ewv{kc}", (kvk, kc, off)], w=[pk])
                self.act(Vt[:, gch, :], ps[:, 0:512], AF.Copy, r=[pk], w=[(Vtk, gch)])
        if os.environ.get("KSTOP") == "K":
            self.p.barrier()
            self.ar.release(m0)
            return
        wq = self.ar.alloc([128, 2, 768], BF16)
        Wo1 = self.ar.alloc([128, 4, 1024], BF16)
        self.load("pool", wq[:], d["wq_b"][j].rearrange("(kc p) n -> p kc n", p=128), "ewq", "ewq")
        self.load("pool", Wo1[:], d["ab_out"][j][0:512, :].rearrange("(kc p) n -> p kc n", p=128), "eWo1", "eWo1")
        rtmp = rtmpB
        rtmp["dbg"] = True
        Qn = self.ar.alloc([128, 4, 512], BF16)
        Qp = self.ar.alloc([64, 4, 512], BF16)
        o = self.ar.alloc([128, 4, 512], BF16)
        Pt = [self.ar.alloc([128, 512], BF16) for _ in range(3)]
        rd = self.ar.alloc([128, 512], F32)
        scale = 192.0 ** -0.5
        pi = 0
        for (off, n, is_x) in c.tiles:
            if os.environ.get("KSTOP") == "B0":
                continue
            if is_x:
                kchunks = list(range(T // 128))
            else:
                kchunks = list(range(S // 128, T // 128))
            for h in range(4):
                ps, pk = self.proj_fm(wq, "ewq", h * 192, 128, qn, qnk, off, n, nk=2)
                self.act(Qn[:, h, 0:n], ps[:, 0:n], AF.Copy, r=[pk], w=[("eQn", h)])
                if os.environ.get("KNOROPE"):
                    continue
                if os.environ.get("KH0") and h > 0:
                    continue
                for _ in range(int(os.environ.get("KSHIFT", "0"))):
                    self.psum("mm")
                ps, pk = self.proj_fm(wq, "ewq", h * 192 + 128, 64, qn, qnk, off, n, nk=2)
                self.rope(ps[0:64, 0:n], pk, off, n, Qp[:, h, 0:n], ("eQp", h), rtmp, 1.0, is_x)
            if os.environ.get("KSTOP") == "B1":
                continue
            for h in range(4):
                pO, pOk = self.psum("acc")
                pD, pDk = self.psum("den")

                def qk(kc):
                    ps, pk = self.psum("mm")
                    koff = (kc * 128) // 512 * 512 if kc * 128 < S else S
                    b2 = os.environ.get("KSTOP") == "B2"
                    self.mm(ps[:, 0:n], Kn[:, h, kc * 128:(kc + 1) * 128], Qn[:, h, 0:n], True, b2, r=[(Knk, h, koff), ("eQn", h)], w=[pk])
                    if not b2:
                        self.mm(ps[:, 0:n], kpe[:, kc * 128:(kc + 1) * 128], Qp[:, h, 0:n], False, True, r=[(kpk, koff), ("eQp", h)], w=[pk])
                    return ps, pk

                nxt = qk(kchunks[0])
                for i, kc in enumerate(kchunks):
                    ps, pk = nxt
                    if i + 1 < len(kchunks):
                        nxt = qk(kchunks[i + 1])
                    P = Pt[pi % 3]
                    Pk = f"ePt{pi % 3}"
                    pi += 1
                    self.act(P[:, 0:n], ps[:, 0:n], AF.Exp, r=[pk], w=[Pk], scale=scale)
                    self.mm(pO[:, 0:n], Vt[:, kc, h * 128:(h + 1) * 128], P[:, 0:n], i == 0, i == len(kchunks) - 1, r=[(Vtk, kc), Pk], w=[pOk])
                    self.mm(pD[:, 0:n], self.ones(128, 128), P[:, 0:n], i == 0, i == len(kchunks) - 1, r=["CSTB", Pk], w=[pDk])
                self.recip(rd[:, 0:n], pD[:, 0:n], r=[pDk], w=["erd"])
                self.tt("dve", o[:, h, 0:n], pO[:, 0:n], rd[:, 0:n], ALU.mult, r=[pOk, "erd"], w=[("eo", h)])
            for cc in range(8):
                po, pok = self.psum("acc")
                for h in range(4):
                    self.mm(po[:, 0:n], Wo1[:, h, cc * 128:(cc + 1) * 128], o[:, h, 0:n], h == 0, h == 3, r=["eWo1", ("eo", h)], w=[pok])
                self.resid(po[:, 0:n], pok, l, 2, b, cc, off, n, is_x)
        self.p.barrier()
        self.ar.release(m0)

    def odd_mixer(self, l, b):
        c = self.cfg
        d = self.d
        j = l // 2
        T, S = c.T, c.S
        last = l == c.DEPTH - 1
        NCH = T // 128
        XCH = S // 128
        CI = d["cd_in"][j].rearrange("(kc p) n -> p kc n", p=128)
        self.p.barrier()
        m0 = self.ar.mark()
        xn = self.ar.alloc([128, 8, T], BF16)
        xk = self.k("oxn")
        mt = self.ar.mark()
        ntmp = self.norm_tmp()
        for (off, n, is_x) in c.tiles:
            self.norm_tile(l, 0, b, off, n, is_x, xn, off, xk, ntmp)
        self.p.barrier()
        self.ar.release(mt)
        CS = self.CST
        RQK, RKQ = CS[:, 320:448], CS[:, 448:576]
        MGE, MLE = CS[:, 576:1088], CS[:, 1088:1600]
        IO1, IO2 = CS[0:64, 1600:1728], CS[0:64, 1728:1856]
        PC = CS[:, 1856:1858]
        dec = self.ar.alloc([128, 8], F32)
        lg = self.ar.alloc([128, 8], F32)
        self.load("sp", dec[:], d["dec_bc"][:, j * 8:(j + 1) * 8], "odec", "odec")
        self.act(lg[:], dec[:], AF.Exp, r=["odec"], w=["olg"], scale=-1.0)
        self.act(lg[:], lg[:], AF.Ln, r=["olg"], w=["olg"], bias=1.0)
        self.I("dve", lambda e: e.tensor_scalar(out=lg[:], in0=lg[:], scalar1=-1.0, scalar2=None, op0=ALU.mult), r=["olg"], w=["olg"])
        Dc = self.ar.alloc([128, 4, 128], F32)
        e1 = self.ar.alloc([128, 128], F32)
        e2 = self.ar.alloc([128, 128], F32)
        qdf = self.ar.alloc([64, 4, 128], F32)
        qdb = self.ar.alloc([64, 4, 128], F32)
        kdec = self.ar.alloc([128, 8], F32)
        cdec = self.ar.alloc([128, 8], F32)
        rnv = self.ar.alloc([128, 4], F32)
        self.load("sp", rnv[:], d["retnT"][:, j * 4:(j + 1) * 4], "ornv", "ornv")
        sinkr = self.ar.alloc([1, 1024], BF16)
        sinkf = self.ar.alloc([1, 1024], F32)
        self.load("sp", sinkf[:], d["sink_row"][:, j * 1024:(j + 1) * 1024], "osinkf", "osinkf")
        self.act(sinkr[:], sinkf[:], AF.Exp, r=["osinkf"], w=["osinkr"])
        for h in range(4):
            self.act(e1[:], RQK, AF.Exp, r=["CST", "olg"], w=["oe1"], scale=lg[:, h:h + 1])
            self.tt("dve", e1[:], e1[:], MGE[:, 0:128], ALU.mult, r=["oe1", "CST"], w=["oe1"])
            self.act(e2[:], RKQ, AF.Exp, r=["CST", "olg"], w=["oe2"], scale=lg[:, 4 + h:5 + h])
            self.tt("dve", e2[:], e2[:], MLE[:, 0:128], ALU.mult, r=["oe2", "CST"], w=["oe2"])
            self.tt("dve", Dc[:, h, :], e1[:], e2[:], ALU.add, r=["oe1", "oe2"], w=["oDc"])
            self.act(qdf[:, h, :], IO1, AF.Exp, r=["CST", "olg"], w=["oqd"], scale=lg[0:64, h:h + 1])
            self.act(qdb[:, h, :], IO2, AF.Exp, r=["CST", "olg"], w=["oqd"], scale=lg[0:64, 4 + h:5 + h])
            self.act(kdec[:, h:h + 1], PC[:, 0:1], AF.Exp, r=["CST", "olg"], w=["okd"], scale=lg[:, h:h + 1])
            self.act(kdec[:, 4 + h:5 + h], PC[:, 1:2], AF.Exp, r=["CST", "olg"], w=["okd"], scale=lg[:, 4 + h:5 + h])
        self.act(cdec[:], lg[:], AF.Exp, r=["olg"], w=["ocd"], scale=128.0)
        m1 = self.ar.mark()
        def ret_head(h, slot):
            K = lambda name: f"{name}@{slot}"
            rtm = self.rope_tmp(1, tag=K("r"))
            wk = self.ar.alloc([128, 8, 64], BF16)
            wv = self.ar.alloc([128, 8, 128], BF16)
            wq = self.ar.alloc([128, 8, 64], BF16)
            wg = self.ar.alloc([128, 8, 128], BF16)
            wo = self.ar.alloc([128, 1024], BF16)
            self.load("pool", wk[:], CI[:, :, h * 64:(h + 1) * 64], K("owk"), K("owk"))
            self.load("pool", wv[:], CI[:, :, 256 + h * 128:256 + (h + 1) * 128], K("owv"), K("owv"))
            self.load("pool", wq[:], CI[:, :, 1024 + h * 64:1024 + (h + 1) * 64], K("owq"), K("owq"))
            self.load("pool", wg[:], CI[:, :, 1280 + h * 128:1280 + (h + 1) * 128], K("owg"), K("owg"))
            self.load("pool", wo[:], d["cd_out"][j][h * 128:(h + 1) * 128, :], K("owo"), K("owo"))
            snaps = self.ar.alloc([64, NCH, 128], BF16)
            Sb = self.ar.alloc([64, 128], F32)
            Sf = self.ar.alloc([64, 128], F32)
            Sfb = self.ar.alloc([64, 128], BF16)
            krT = self.ar.alloc([64, 512], BF16)
            qrT = self.ar.alloc([64, 512], BF16)
            qf = self.ar.alloc([64, 512], BF16)
            qb = self.ar.alloc([64, 512], BF16)
            vt = self.ar.alloc([128, 4, 128], BF16)
            kd = self.ar.alloc([128, 4, 64], BF16)
            sg = self.ar.alloc([128, 512], BF16)
            At = [self.ar.alloc([128, 128], BF16) for _ in range(2)]
            yf = self.ar.alloc([128, 512], F32)
            ysq = self.ar.alloc([128, 512], BF16)
            rs = self.ar.alloc([128, 512], F32)
            yg = ysq
            snk = self.k("snap")

            def kv_tile(off, n, is_x, dirn):
                ps, pk = self.proj_fm(wk, K("owk"), 0, 64, xn, xk, off, n)
                self.rope(ps[0:64, 0:n], pk, off, n, krT[:, 0:n], K("okrT"), rtm, 0.125, is_x)
                for ch in range(n // 128):
                    ps, pk = self.psum("mm")
                    for kc in range(8):
                        self.mm(ps[:, 0:128], xn[:, kc, off + ch * 128:off + (ch + 1) * 128], wv[:, kc, :], kc == 0, kc == 7, r=[K("owv"), (xk, kc, off)], w=[pk])
                    self.act(vt[:, ch, :], ps[:, 0:128], AF.Copy, r=[pk], w=[(K("ovt"), ch)])
                    self.I("pe", (lambda ch, krT: lambda e: e.transpose(self.pst[:, ch * 64:(ch + 1) * 64], krT[:, ch * 128:(ch + 1) * 128], self.ident(64)))(ch, krT),
                           r=[K("okrT"), "CSTB"], w=["pst"])
                    col = h if dirn == 0 else 4 + h
                    self.I("dve", (lambda ch, col, kd, kdec: lambda e: e.tensor_scalar(out=kd[:, ch, :], in0=self.pst[:, ch * 64:(ch + 1) * 64], scalar1=kdec[:, col:col + 1],
                                                                                         scalar2=None, op0=ALU.mult))(ch, col, kd, kdec), r=["pst", "okd"], w=[(K("okdt"), ch)])

            def state_update(St, Sk, ch, dirn):
                ps, pk = self.psum("den")
                self.mm(ps[0:64, 0:128], kd[:, ch, :], vt[:, ch, :], True, True, r=[(K("okdt"), ch), (K("ovt"), ch)], w=[pk])
                col = h if dirn == 0 else 4 + h
                self.stt("dve", St[:], St[:], cdec[0:64, col:col + 1], ps[0:64, 0:128], ALU.mult, ALU.add, r=[Sk, "ocd", pk], w=[Sk])

            self.I("dve", (lambda t: lambda e: e.memset(t[:], 0.0))(Sb), w=[K("oSb")])
            self.I("dve", (lambda t: lambda e: e.memset(t[:], 0.0))(Sf), w=[K("oSf")])
            self.I("dve", (lambda t: lambda e: e.memset(t[:], 0.0))(Sfb), w=[K("oSfb")])
            btiles = [c.tiles[-1]] + list(reversed(c.tiles[:-1]))
            for (off, n, is_x) in btiles:
                kv_tile(off, n, is_x, 1)
                yield
                for ch in reversed(range(n // 128)):
                    gch = off // 128 + ch
                    self.act(snaps[:, gch, :], Sb[:], AF.Copy, r=[K("oSb")], w=[(snk, gch)])
                    state_update(Sb, K("oSb"), ch, 1)
                    yield
            ftiles = [c.tiles[-1]] + list(c.tiles[:-1])
            for (off, n, is_x) in ftiles:
                kv_tile(off, n, is_x, 0)
                yield
                want_out = is_x or not last
                nch = n // 128
                if want_out:
                    ps, pk = self.proj_fm(wq, K("owq"), 0, 64, xn, xk, off, n)
                    self.rope(ps[0:64, 0:n], pk, off, n, qrT[:, 0:n], K("oqrT"), rtm, 1.0, is_x)
                    for ch in range(nch):
                        self.tt("dve", qf[:, ch * 128:(ch + 1) * 128], qrT[:, ch * 128:(ch + 1) * 128], qdf[:, h, :], ALU.mult, r=[K("oqrT"), "oqd"], w=[(K("oqf"), ch)])
                        self.tt("pool", qb[:, ch * 128:(ch + 1) * 128], qrT[:, ch * 128:(ch + 1) * 128], qdb[:, h, :], ALU.mult, r=[K("oqrT"), "oqd"], w=[(K("oqb"), ch)])
                    ps, pk = self.proj_fm(wg, K("owg"), 0, 128, xn, xk, off, n)
                    self.act(sg[:, 0:n], ps[:, 0:n], AF.Silu, r=[pk], w=[K("osg")])
                    yield
                    pY, pYk = self.psum("acc")
                for ch in range(nch):
                    gch = off // 128 + ch
                    if want_out:
                        ps, pk = self.psum("mm")
                        self.mm(ps[:, 0:128], krT[:, ch * 128:(ch + 1) * 128], qrT[:, ch * 128:(ch + 1) * 128], True, True, r=[K("okrT"), K("oqrT")], w=[pk])
                        A = At[ch % 2]
                        Ak = K(f"oAt{ch % 2}")
                        self.tt("dve", A[:], ps[:, 0:128], Dc[:, h, :], ALU.mult, r=[pk, "oDc"], w=[Ak])
                        ysl = pY[:, ch * 128:(ch + 1) * 128]
                        self.mm(ysl, vt[:, ch, :], A[:], True, False, r=[(K("ovt"), ch), Ak], w=[pYk])
                        self.mm(ysl, Sfb[:], qf[:, ch * 128:(ch + 1) * 128], False, False, r=[K("oSfb"), (K("oqf"), ch)], w=[pYk])
                        self.mm(ysl, snaps[:, gch, :], qb[:, ch * 128:(ch + 1) * 128], False, True, r=[(snk, gch), (K("oqb"), ch)], w=[pYk])
                    state_update(Sf, K("oSf"), ch, 0)
                    self.act(Sfb[:], Sf[:], AF.Copy, r=[K("oSf")], w=[K("oSfb")])
                    yield
                if not want_out:
                    continue
                self.act(yf[:, 0:n], pY[:, 0:n], AF.Copy, r=[pYk], w=[K("oyf")])
                self.act(ysq[:, 0:n], pY[:, 0:n], AF.Square, r=[pYk], w=[K("oysq")])
                pd, pdk = self.psum("den")
                self.mm(pd[:, 0:n], self.ones(128, 128), ysq[:, 0:n], True, True, r=[K("oysq"), "CSTB"], w=[pdk])
                self.rstd(pd[:, 0:n], rs[:, 0:n], 1.0 / 128, r=[pdk], w=[K("ors")])
                self.stt("dve", yf[:, 0:n], yf[:, 0:n], rnv[:, h:h + 1], rs[:, 0:n], ALU.mult, ALU.mult, r=[K("oyf"), "ornv", K("ors")], w=[K("oyf")])
                self.tt("pool", yg[:, 0:n], yf[:, 0:n], sg[:, 0:n], ALU.mult, r=[K("oyf"), K("osg")], w=[K("oysq")])
                yield
                for cc in range(8):
                    po, pok = self.psum("acc")
                    self.mm(po[:, 0:n], wo[:, cc * 128:(cc + 1) * 128], yg[:, 0:n], True, True, r=[K("owo"), K("oysq")], w=[pok])
                    self.resid(po[:, 0:n], pok, l, 2, b, cc, off, n, is_x)
                yield

        for pair in ((0, 1), (2, 3)):
            self.ar.release(m1)
            gens = [ret_head(hh, sl) for sl, hh in enumerate(pair)]
            while gens:
                for g_ in list(gens):
                    try:
                        next(g_)
                    except StopIteration:
                        gens.remove(g_)
            self.p.barrier()
        self.ar.release(m1)
        rtmp = self.rope_tmp()
        for g in range(2):
            self.ar.release(m1)
            rtmp = self.rope_tmp()
            wk = self.ar.alloc([128, 8, 64], BF16)
            wv = self.ar.alloc([128, 8, 64], BF16)
            wq = self.ar.alloc([128, 8, 256], BF16)
            wo = self.ar.alloc([64, 4, 1024], BF16)
            self.load("pool", wk[:], CI[:, :, 768 + g * 64:768 + (g + 1) * 64], "owk", "owk")
            self.load("pool", wv[:], CI[:, :, 896 + g * 64:896 + (g + 1) * 64], "owv", "owv")
            self.load("pool", wq[:], CI[:, :, 1792 + g * 256:1792 + (g + 1) * 256], "owq", "owq")
            self.load("pool", wo[:], d["cd_out"][j][512 + g * 256:512 + (g + 1) * 256, :].rearrange("(hh dd) n -> dd hh n", dd=64), "owo2", "owo2")
            sK = self.ar.alloc([64, T], BF16)
            sV = self.ar.alloc([128, NCH, 64], BF16)
            sKk, sVk = self.k("sK"), self.k("sV")
            sqf = self.ar.alloc([64, 2048], BF16)
            sq = sqf[:, :].rearrange("p (bb hh q) -> p bb hh q", bb=4, hh=4)
            os_ = self.ar.alloc([64, 4, 512], BF16)
            Pt = [self.ar.alloc([128, 512], BF16) for _ in range(3)]
            rd = self.ar.alloc([64, 512], F32)
            for (off, n, is_x) in c.tiles:
                ps, pk = self.proj_fm(wk, "owk", 0, 64, xn, xk, off, n)
                self.rope(ps[0:64, 0:n], pk, off, n, sK[:, off:off + n], (sKk, off), rtmp, 1.0, is_x)
                for ch in range(n // 128):
                    gch = off // 128 + ch
                    ps, pk = self.psum("mm")
                    for kc in range(8):
                        self.mm(ps[:, 0:64], xn[:, kc, gch * 128:(gch + 1) * 128], wv[:, kc, :], kc == 0, kc == 7, r=["owv", (xk, kc, off)], w=[pk])
                    self.act(sV[:, gch, :], ps[:, 0:64], AF.Copy, r=[pk], w=[(sVk, gch)])
            pi = 0
            for (off, n, is_x) in c.tiles:
                if (not is_x) and last:
                    continue
                nblk = n // 128
                for hh in range(4):
                    ps, pk = self.proj_fm(wq, "owq", hh * 64, 64, xn, xk, off, n)
                    self.rope(ps[0:64, 0:n], pk, off, n, sq[:, 0:nblk, hh, :], ("osq", hh), rtmp, 1.0, is_x)
                for blk in range(nblk):
                    gi = off // 128 + blk
                    if is_x:
                        kl = []
                        if gi - 1 >= 0:
                            kl.append((gi - 1, MLE))
                        kl.append((gi, None))
                        if gi + 1 < XCH:
                            kl.append((gi + 1, MGE))
                        kl += [(XCH, None), (XCH + 1, None)]
                    else:
                        kl = [(XCH, None), (XCH + 1, None)]
                    pO, pOk = self.psum("acc")
                    pD, pDk = self.psum("den")
                    qrhs = sqf[:, blk * 512:(blk + 1) * 512]
                    def qk(kc):
                        koff = (kc * 128) // 512 * 512 if kc < XCH else S
                        ps, pk = self.psum("mm")
                        self.mm(ps[:, 0:512], sK[:, kc * 128:(kc + 1) * 128], qrhs, True, True, r=[(sKk, koff)] + [("osq", hh) for hh in range(4)], w=[pk])
                        return ps, pk

                    nxt = qk(kl[0][0])
                    for i, (kc, msk) in enumerate(kl):
                        ps, pk = nxt
                        if i + 1 < len(kl):
                            nxt = qk(kl[i + 1][0])
                        P = Pt[pi % 3]
                        Pk = f"oPt{pi % 3}"
                        pi += 1
                        self.act(P[:], ps[:, 0:512], AF.Exp, r=[pk], w=[Pk], scale=0.125)
                        if msk is not None:
                            self.tt("pool", P[:], P[:], msk, ALU.mult, r=[Pk, "CST"], w=[Pk])
                        self.mm(pO[0:64, 0:512], sV[:, kc, :], P[:], i == 0, i == len(kl) - 1, r=[(sVk, kc), Pk], w=[pOk])
                        self.mm(pD[0:64, 0:512], self.ones(128, 64), P[:], i == 0, False, r=["CSTB", Pk], w=[pDk])
                    self.mm(pD[0:64, 0:512], self.ones(1, 64), sinkr[0:1, g * 512:(g + 1) * 512], False, True, r=["CSTB", "osinkr"], w=[pDk])
                    self.recip(rd[:], pD[0:64, 0:512], r=[pDk], w=["ord"])
                    self.tt("dve", os_[:, :, blk * 128:(blk + 1) * 128], pO[0:64, 0:512].rearrange("p (hh q) -> p hh q", hh=4), rd[:].rearrange("p (hh q) -> p hh q", hh=4),
                            ALU.mult, r=[pOk, "ord"], w=[("oos", blk)])
                for cc in range(8):
                    po, pok = self.psum("acc")
                    for hh in range(4):
                        self.mm(po[:, 0:n], wo[:, hh, cc * 128:(cc + 1) * 128], os_[:, hh, 0:n], hh == 0, hh == 3, r=["owo2"] + [("oos", bb) for bb in range(nblk)], w=[pok])
                    self.resid(po[:, 0:n], pok, l, 2, b, cc, off, n, is_x)
            self.p.barrier()
        self.ar.release(m0)

    def build(self):
        c = self.cfg
        d = self.d
        S, T = c.S, c.T
        self.setup()
        allx = [("XH", cc, off) for cc in range(8) for (off, n, is_x) in c.tiles if is_x]
        allh = [("XH", cc, S) for cc in range(8)]
        for b in range(c.NB):
            self.load("sp", self.XH[:, :, 0:S], d["xT"][b].rearrange("(c p) t -> p c t", p=128), None, "xload", keys=allx)
            self.load("sp", self.XH[:, :, S:T], d["ctxT"][b].rearrange("(c p) t -> p c t", p=128), None, "hload", keys=allh)
            import os
            skip = os.environ.get("KSKIP", "").split(",")
            for l in range(c.DEPTH):
                if l % 2 == 0:
                    if "even" not in skip:
                        self.even_mixer(l, b)
                else:
                    if "odd" not in skip:
                        self.odd_mixer(l, b)
                if "ffn" not in skip:
                    self.ffn(l, b)
            m = self.ar.mark()
            sq = [self.ar.alloc([128, 512], BF16) for _ in range(2)]
            rs = self.ar.alloc([128, 512], F32)
            ob = [self.ar.alloc([128, 8, 512], F32) for _ in range(2)]
            oT = self.outT[b].rearrange("(c p) t -> p c t", p=128)
            for ti, (off, n, is_x) in enumerate(c.tiles):
                if not is_x:
                    continue
                pd, pk = self.psum("den")
                for cc in range(8):
                    s = sq[cc % 2]
                    sk = f"nsq{cc % 2}"
                    self.act(s[:, 0:n], self.XH[:, cc, off:off + n], AF.Square, r=[("XH", cc, off)], w=[sk])
                    self.mm(pd[:, 0:n], self.ones(128, 128), s[:, 0:n], cc == 0, cc == 7, r=[sk, "CSTB"], w=[pk])
                self.rstd(pd[:, 0:n], rs[:, 0:n], 1.0 / D, r=[pk], w=["nrs"])
                o = ob[ti % 2]
                ok = f"ob{ti % 2}"
                for cc in range(8):
                    self.stt("dve", o[:, cc, 0:n], self.XH[:, cc, off:off + n], self.NFIN[:, cc:cc + 1], rs[:, 0:n], ALU.mult, ALU.mult,
                             r=[("XH", cc, off), "NFIN", "nrs"], w=[(ok, cc)])
                for cc in range(8):
                    self.I("sp", (lambda o, off, n, cc, oT: lambda e: e.dma_start(out=oT[:, cc, off:off + n], in_=o[:, cc, 0:n]))(o, off, n, cc, oT),
                           r=[(ok, cc)], dsem=ok, cont=(cc > 0))
            self.p.barrier()
            self.ar.release(m)
        self.p.barrier()
        self.p.emit()
        return self.nc


def _consts(S):
    cst = np.zeros((128, 2048), np.float32)
    cst[:, 0:128] = 1.0
    cst[:, 128:256] = np.eye(128, dtype=np.float32)
    for i in range(32):
        cst[32 + i, 256 + i] = -1.0
        cst[i, 256 + 32 + i] = 1.0
    k = np.arange(128)[:, None].astype(np.float32)
    q = np.arange(128)[None, :].astype(np.float32)
    cst[:, 320:448] = np.maximum(q - k, 0)
    cst[:, 448:576] = np.maximum(k - q, 0)
    cst[:, 576:1088] = np.tile((q >= k).astype(np.float32), (1, 4))
    cst[:, 1088:1600] = np.tile((k >= q).astype(np.float32), (1, 4))
    cst[:, 1600:1728] = np.arange(128, dtype=np.float32)[None, :] + 1.0
    cst[:, 1728:1856] = 128.0 - np.arange(128, dtype=np.float32)[None, :]
    cst[:, 1856] = 127.0 - np.arange(128, dtype=np.float32)
    cst[:, 1857] = np.arange(128, dtype=np.float32)
    t = np.arange(S)
    row = (t // 64).astype(np.float32)
    col = (t % 64).astype(np.float32)
    freqs = (np.float32(10000.0) ** (-np.arange(16, dtype=np.float32) / np.float32(16))).astype(np.float32)
    ang = np.concatenate([row[:, None] * freqs, col[:, None] * freqs], axis=-1).astype(np.float32)
    C = np.cos(ang).astype(np.float32).T
    Sn = np.sin(ang).astype(np.float32).T
    return cst, np.ascontiguousarray(np.concatenate([C, C], 0)), np.ascontiguousarray(np.concatenate([Sn, Sn], 0))


def _fm(v):
    v = np.asarray(v, np.float32)
    v2 = v.reshape(-1, v.shape[-1] // 128, 128)
    return np.ascontiguousarray(v2.transpose(2, 0, 1).reshape(128, -1))


_CACHE = {}


def run(inputs, S, NBTOT, DEPTH, n_cores):
    NB = NBTOT // n_cores
    cfg = Cfg(S=S, NB=NB, DEPTH=DEPTH)
    key = (S, NB, DEPTH)
    if key not in _CACHE:
        _CACHE[key] = Builder(cfg).build()
    nc = _CACHE[key]
    f = lambda a: np.ascontiguousarray(np.asarray(a, np.float32))
    NE, NO = cfg.NE, max(cfg.NO, 1)
    cst, rC, rS = _consts(S)
    x, c, ctx, c_ctx = inputs["x"], inputs["c"], inputs["ctx"], inputs["c_ctx"]
    shared = dict(
        ada_w=f(inputs["ada_w"]), ada_bT=_fm(inputs["ada_b"]), nmixT=_fm(inputs["norm_mix"]), nffnT=_fm(inputs["norm_ffn"]),
        nfinT=_fm(inputs["norm_final"]), ffn_in=f(inputs["ffn_in"]), ffn_out=f(inputs["ffn_out"]),
        ab_in=f(inputs["ab_in"]), ab_out=f(inputs["ab_out"]), qnT=_fm(inputs["mla_q_norm"]), kvnT=_fm(inputs["mla_kv_norm"]),
        wq_b=f(inputs["mla_wq_b"]), wkv_b=f(inputs["mla_wkv_b"]),
        vnorm_bc=np.ascontiguousarray(np.broadcast_to(f(inputs["cmlp_v_norm"]).reshape(1, -1), (128, NE * 512))),
        wsT=np.ascontiguousarray(f(inputs["cmlp_ws"]).transpose(0, 3, 1, 2).reshape(NE, 128, 512)),
        bs_row=f(inputs["cmlp_bs"]).reshape(1, -1),
        cd_in=f(inputs["cd_in"]), cd_out=f(inputs["cd_out"]),
        dec_bc=np.ascontiguousarray(np.broadcast_to(np.concatenate([f(inputs["ret_decay_fwd"]), f(inputs["ret_decay_bwd"])], axis=1).reshape(1, -1), (128, NO * 8))),
        retnT=_fm(inputs["ret_norm"]),
        sink_row=np.ascontiguousarray(np.repeat(f(inputs["swa_sink"]).reshape(-1), 128).reshape(1, -1)),
        ropeC=rC, ropeS=rS, cst=cst,
    )
    in_maps = []
    for i in range(n_cores):
        sl = slice(i * NB, (i + 1) * NB)
        m = dict(shared)
        m["xT"] = np.ascontiguousarray(f(x[sl]).transpose(0, 2, 1))
        m["ctxT"] = np.ascontiguousarray(f(ctx[sl]).transpose(0, 2, 1))
        m["cT"] = np.ascontiguousarray(np.concatenate([f(c[sl]), f(c_ctx)[None, :]], axis=0).T)
        in_maps.append(m)
    res = run_bass_kernel_spmd(nc, in_maps, core_ids=list(range(n_cores)))
    outs = [np.asarray(r["outT"]).transpose(0, 2, 1) for r in res.results]
    return np.ascontiguousarray(np.concatenate(outs, axis=0)).astype(np.float32)


def kernel(**inputs):
    return run(inputs, 2048, 32, 4, 8)
```
